# Optimizing a Trainium2 kernel written in Bass

```python
import jax
import jax.numpy as jnp
from jax import lax
import numpy as np

D_MODEL = 4096
BATCH = 2
SEQ = 8192
DEPTH = 1

CHUNK = 64
MIX_WIDTH = D_MODEL
GMLP_WIDTH = MIX_WIDTH // 2
GMLP_HEADS = 16
GMLP_HEAD_CH = GMLP_WIDTH // GMLP_HEADS
GMLP_BLOCK = 128
RWKV_WIDTH = MIX_WIDTH - GMLP_WIDTH
RWKV_HEAD_DIM = 64
RWKV_HEADS = RWKV_WIDTH // RWKV_HEAD_DIM
DECAY_LORA = 64
ICLR_LORA = 64
GATE_LORA = 256
SHIFT_WIDTH = 3 * RWKV_WIDTH + DECAY_LORA + ICLR_LORA + GATE_LORA
PROJ_WIDTH = 2 * GMLP_WIDTH + SHIFT_WIDTH
D_FF = 4 * D_MODEL
RMS_EPS = 1e-6
LN_EPS = 1e-5
GN_EPS = RWKV_HEAD_DIM * 1e-5
L2_EPS = 1e-12

kernel_name = 'hybrid_gmlp_rwkv7_sandwich_block'


def rms_norm(x, g):
    xf = x.astype(jnp.float32)
    y = xf * lax.rsqrt(jnp.mean(xf * xf, axis=-1, keepdims=True) + RMS_EPS)
    return (y * g.astype(jnp.float32)).astype(x.dtype)


def layer_norm(x, g, b):
    xf = x.astype(jnp.float32)
    xc = xf - jnp.mean(xf, axis=-1, keepdims=True)
    y = xc * lax.rsqrt(jnp.mean(xc * xc, axis=-1, keepdims=True) + LN_EPS)
    return (y * g.astype(jnp.float32) + b.astype(jnp.float32)).astype(x.dtype)


def block_causal_mask():
    pos = jnp.arange(GMLP_BLOCK)
    return (pos[None, :] // CHUNK) <= (pos[:, None] // CHUNK)


def gmlp_spatial_gating(p_g, ln_g, ln_b, ws, bs):
    B, T, _ = p_g.shape
    z = jax.nn.gelu(p_g)
    u, v = z[..., :GMLP_WIDTH], z[..., GMLP_WIDTH:]
    v = layer_norm(v, ln_g, ln_b)
    vb = v.reshape(B, T // GMLP_BLOCK, GMLP_BLOCK, GMLP_HEADS, GMLP_HEAD_CH)
    ws_m = jnp.where(block_causal_mask()[None], ws, jnp.zeros((), ws.dtype))
    mixed = jnp.einsum('hij,bnjhc->bnihc', ws_m, vb) + bs.T[:, :, None]
    return u * mixed.reshape(B, T, GMLP_WIDTH)


def rwkv7_scan(r, w, k, v, kk, a):
    B, T, H, N = r.shape

    def step(S, inp):
        r_t, w_t, k_t, v_t, kk_t, a_t = inp
        sa = jnp.einsum('bhvk,bhk->bhv', S, -kk_t)
        S = (S * w_t[:, :, None, :]
             + sa[..., None] * (kk_t * a_t)[:, :, None, :]
             + v_t[..., None] * k_t[:, :, None, :])
        return S, jnp.einsum('bhvk,bhk->bhv', S, r_t)

    xs = tuple(jnp.moveaxis(t.astype(jnp.float32), 1, 0) for t in (r, w, k, v, kk, a))
    S0 = jnp.zeros((B, H, N, N), jnp.float32)
    _, y = lax.scan(step, S0, xs)
    return jnp.moveaxis(y, 0, 1)


def rwkv7_time_mix(p_s, mu, w0, w_up, a0, a_up, g_up, k_k, k_a, r_k, lnx_g, lnx_b):
    B, T, _ = p_s.shape
    H, N = RWKV_HEADS, RWKV_HEAD_DIM
    p_prev = jnp.pad(p_s[:, :-1], ((0, 0), (1, 0), (0, 0)))
    p_s = p_s + (p_prev - p_s) * mu
    cuts = [RWKV_WIDTH, 2 * RWKV_WIDTH, 3 * RWKV_WIDTH,
            3 * RWKV_WIDTH + DECAY_LORA, 3 * RWKV_WIDTH + DECAY_LORA + ICLR_LORA]
    r, k, v, xw, xa, xg = jnp.split(p_s, cuts, axis=-1)
    w_log = -jax.nn.softplus(-(w0 + jnp.tanh(xw) @ w_up)) - 0.5
    decay = jnp.exp(-jnp.exp(w_log.astype(jnp.float32)))
    a = jax.nn.sigmoid(a0 + xa @ a_up)
    g = jax.nn.sigmoid(xg) @ g_up
    r, k, v, decay, a = (t.reshape(B, T, H, N) for t in (r, k, v, decay, a))
    kk = k.astype(jnp.float32) * k_k.reshape(H, N).astype(jnp.float32)
    kk = kk / jnp.maximum(jnp.sqrt(jnp.sum(kk * kk, axis=-1, keepdims=True)), L2_EPS)
    k = k * (1 + (a - 1) * k_a.reshape(H, N))
    y = rwkv7_scan(r, decay, k, v, kk, a)
    yc = y - jnp.mean(y, axis=-1, keepdims=True)
    y = yc * lax.rsqrt(jnp.mean(yc * yc, axis=-1, keepdims=True) + GN_EPS)
    y = y * lnx_g.reshape(H, N).astype(jnp.float32) + lnx_b.reshape(H, N).astype(jnp.float32)
    bonus = jnp.sum((r * k * r_k).astype(jnp.float32), axis=-1, keepdims=True)
    y = y + bonus * v.astype(jnp.float32)
    return (y.reshape(B, T, RWKV_WIDTH) * g.astype(jnp.float32)).astype(p_s.dtype)


def setup_inputs(seed: int = 0) -> dict:
    key = jax.random.key(seed)
    ks = jax.random.split(key, 24)
    f32 = jnp.float32
    L = DEPTH

    def nrm(k, shape, s):
        return s * jax.random.normal(k, shape, f32)

    return {
        'x': jax.random.normal(ks[0], (BATCH, SEQ, D_MODEL), f32),
        'pre_mix_g': 1.0 + nrm(ks[1], (L, D_MODEL), 0.05),
        'w_in': nrm(ks[2], (L, D_MODEL, PROJ_WIDTH), D_MODEL ** -0.5),
        'tshift_mu': jax.random.uniform(ks[3], (L, SHIFT_WIDTH), f32),
        'gmlp_ln_g': 1.0 + nrm(ks[4], (L, GMLP_WIDTH), 0.05),
        'gmlp_ln_b': nrm(ks[5], (L, GMLP_WIDTH), 0.02),
        'gmlp_ws': nrm(ks[6], (L, GMLP_HEADS, GMLP_BLOCK, GMLP_BLOCK), GMLP_BLOCK ** -0.5),
        'gmlp_bs': 1.0 + nrm(ks[7], (L, GMLP_HEADS, GMLP_BLOCK), 0.1),
        'decay_w0': 0.5 + nrm(ks[8], (L, RWKV_WIDTH), 1.0),
        'decay_up': nrm(ks[9], (L, DECAY_LORA, RWKV_WIDTH), 0.5 * DECAY_LORA ** -0.5),
        'iclr_a0': nrm(ks[10], (L, RWKV_WIDTH), 0.5),
        'iclr_up': nrm(ks[11], (L, ICLR_LORA, RWKV_WIDTH), ICLR_LORA ** -0.5),
        'gate_up': nrm(ks[12], (L, GATE_LORA, RWKV_WIDTH), GATE_LORA ** -0.5),
        'k_k': 0.85 + nrm(ks[13], (L, RWKV_WIDTH), 0.05),
        'k_a': 1.0 + nrm(ks[14], (L, RWKV_WIDTH), 0.05),
        'r_k': nrm(ks[15], (L, RWKV_HEADS, RWKV_HEAD_DIM), 0.1),
        'lnx_g': 1.0 + nrm(ks[16], (L, RWKV_WIDTH), 0.05),
        'lnx_b': nrm(ks[17], (L, RWKV_WIDTH), 0.02),
        'w_out': nrm(ks[18], (L, MIX_WIDTH, D_MODEL), MIX_WIDTH ** -0.5),
        'post_mix_g': 1.0 + nrm(ks[19], (L, D_MODEL), 0.05),
        'pre_ffn_g': 1.0 + nrm(ks[20], (L, D_MODEL), 0.05),
        'w_ff1': nrm(ks[21], (L, D_MODEL, D_FF), D_MODEL ** -0.5),
        'w_ff2': nrm(ks[22], (L, D_FF, D_MODEL), D_FF ** -0.5),
        'post_ffn_g': 1.0 + nrm(ks[23], (L, D_MODEL), 0.05),
    }


def reference(x, pre_mix_g, w_in, tshift_mu, gmlp_ln_g, gmlp_ln_b, gmlp_ws, gmlp_bs,
              decay_w0, decay_up, iclr_a0, iclr_up, gate_up, k_k, k_a, r_k, lnx_g, lnx_b,
              w_out, post_mix_g, pre_ffn_g, w_ff1, w_ff2, post_ffn_g):
    for l in range(DEPTH):
        h = rms_norm(x, pre_mix_g[l])
        p = h @ w_in[l]
        y_a = gmlp_spatial_gating(p[..., :2 * GMLP_WIDTH], gmlp_ln_g[l], gmlp_ln_b[l],
                                  gmlp_ws[l], gmlp_bs[l])
        y_b = rwkv7_time_mix(p[..., 2 * GMLP_WIDTH:], tshift_mu[l], decay_w0[l], decay_up[l],
                             iclr_a0[l], iclr_up[l], gate_up[l], k_k[l], k_a[l], r_k[l],
                             lnx_g[l], lnx_b[l])
        mix = jnp.concatenate([y_a, y_b], axis=-1) @ w_out[l]
        x = x + rms_norm(mix, post_mix_g[l])
        h = rms_norm(x, pre_ffn_g[l])
        f = jnp.square(jax.nn.relu(h @ w_ff1[l])) @ w_ff2[l]
        x = x + rms_norm(f, post_ffn_g[l])
    return x
```

```python
import numpy as np
from contextlib import ExitStack
import concourse.bass as bass
import concourse.mybir as mybir
from concourse.bass_utils import run_bass_kernel_spmd

F32 = mybir.dt.float32
BF16 = mybir.dt.bfloat16
ALU = mybir.AluOpType
AF = mybir.ActivationFunctionType
AX = mybir.AxisListType

D = 4096
DFF = 16384
KC = 32
NCORE = 8
USE_F32R = False


class Prog:
    ENG = ('pe', 'act', 'dve', 'pool', 'sp')

    def __init__(self, nc, es):
        self.nc = nc
        self.es = es
        self.q = {e: [] for e in self.ENG}
        self.cnt = {e: 0 for e in self.ENG}
        self.sem = {e: es.enter_context(nc.semaphore("s_" + e)) for e in ('pe', 'act', 'dve', 'pool')}
        self.seen = {e: {} for e in self.ENG}
        self.state = {}
        self.dmas = {}
        self.psum_keys = set(['psT0', 'psT1', 'ps2', 'ps3', 'ps4', 'ps5', 'ps6', 'ps7', 'q0', 'q1', 'q2', 'q3', 'q4', 'q5', 'q6', 'q7'])

    def _deps(self, eng, reads, writes):
        deps = {}

        def add(tok):
            if tok is None:
                return
            s, v = tok
            if eng == 'pe' and s == 'pe':
                return
            if deps.get(s, 0) < v:
                deps[s] = v
        for k in reads:
            st = self.state.get(k)
            if st:
                add(st['w'])
                if k in self.psum_keys:
                    for rk, tok in st['r'].items():
                        if rk != eng:
                            add(tok)
        for k in writes:
            st = self.state.get(k)
            if st:
                add(st['w'])
                for rk, tok in st['r'].items():
                    add(tok)
        return deps

    def _emit_waits(self, eng, deps):
        for s, v in deps.items():
            if self.seen[eng].get(s, 0) >= v:
                continue
            self.seen[eng][s] = v
            semh = self.sem[s] if s in self.sem else self.dmas[s][0]
            self.q[eng].append(lambda e, semh=semh, v=v: e.wait_ge(semh, v))

    def _update(self, eng_key, tok, reads, writes):
        for k in reads:
            st = self.state.setdefault(k, {'w': None, 'r': {}})
            st['r'][eng_key] = tok
        for k in writes:
            self.state[k] = {'w': tok, 'r': {}}

    def op(self, eng, name, reads=(), writes=(), inc=True, **kw):
        deps = self._deps(eng, reads, writes)
        self._emit_waits(eng, deps)
        if inc:
            self.cnt[eng] += 1
            tok = (eng, self.cnt[eng])
            semh = self.sem[eng]
            self.q[eng].append(lambda e, name=name, kw=kw, semh=semh: getattr(e, name)(**kw).then_inc(semh, 1))
        else:
            assert eng == 'pe'
            tok = (eng, self.cnt[eng] + 1)
            self.q[eng].append(lambda e, name=name, kw=kw: getattr(e, name)(**kw))
        self._update(eng, tok, reads, writes)
        return tok

    def dma(self, eng, chan, reads=(), writes=(), **kw):
        return self.dma_start(eng, chan, lambda e, kw=kw: e.dma_start(**kw), reads, writes)

    def dma_start(self, eng, chan, fn, reads=(), writes=()):
        s = 'dma:' + chan
        if s not in self.dmas:
            self.dmas[s] = [self.es.enter_context(self.nc.semaphore("d_" + chan)), 0]
        deps = self._deps(eng, reads, writes)
        prev = self.dmas[s][1]
        if prev > 0 and deps.get(s, 0) < prev:
            deps[s] = prev
        self._emit_waits(eng, deps)
        self.dmas[s][1] += 16
        tok = (s, self.dmas[s][1])
        semh = self.dmas[s][0]
        self.q[eng].append(lambda e, fn=fn, semh=semh: fn(e).then_inc(semh, 16))
        self._update(s, tok, reads, writes)
        return tok

    def wait_all(self, eng, keys):
        self._emit_waits(eng, self._deps(eng, keys, keys))

    def alias(self, new_keys, old_keys):
        toks = {}
        for k in old_keys:
            st = self.state.get(k)
            if not st:
                continue
            for tok in [st['w']] + list(st['r'].values()):
                if tok is None:
                    continue
                if toks.get(tok[0], 0) < tok[1]:
                    toks[tok[0]] = tok[1]
        for k in new_keys:
            st = self.state.setdefault(k, {'w': None, 'r': {}})
            for i, (s, v) in enumerate(toks.items()):
                key = '_alias_' + s
                if st['r'].get(key, (s, 0))[1] < v:
                    st['r'][key] = (s, v)

    def barrier(self):
        deps = {e: self.cnt[e] for e in ('pe', 'act', 'dve', 'pool') if self.cnt[e] > 0}
        for s, (h, c) in self.dmas.items():
            if c > 0:
                deps[s] = c
        for e in self.ENG:
            d = {s: v for s, v in deps.items() if s != e}
            self._emit_waits(e, d)

    def replay(self):
        nc = self.nc
        q = self.q
        with nc.Block() as block:
            @block.tensor
            def _(e):
                for f in q['pe']:
                    f(e)

            @block.scalar
            def _(e):
                for f in q['act']:
                    f(e)

            @block.vector
            def _(e):
                for f in q['dve']:
                    f(e)

            @block.gpsimd
            def _(e):
                for f in q['pool']:
                    f(e)

            @block.sync
            def _(e):
                for f in q['sp']:
                    f(e)


class Arena:
    def __init__(self, ap, nwords):
        self.a = ap
        self.n = nwords
        self.off = 0

    def _shape(self, ap, shape):
        if len(shape) == 1:
            return ap
        if len(shape) == 2:
            return ap.rearrange("p (a b) -> p a b", a=shape[0], b=shape[1])
        return ap.rearrange("p (a b c) -> p a b c", a=shape[0], b=shape[1], c=shape[2])

    def f32(self, *shape):
        n = int(np.prod(shape))
        ap = self.a[:, self.off:self.off + n]
        self.off += n
        assert self.off <= self.n, (self.off, self.n)
        return self._shape(ap, shape)

    def bf16(self, *shape):
        n = int(np.prod(shape))
        w = (n + 1) // 2
        ap = self.a[:, self.off:self.off + w].bitcast(BF16)
        self.off += w
        assert self.off <= self.n, (self.off, self.n)
        return self._shape(ap, shape)


PP_GPRE = 0
PP_GFFN = 32
PP_MU = 64
PP_W0 = 79
PP_A0 = 83
PP_KK = 87
PP_KA = 91
PP_RK = 95
PP_BS = 99
NPP = 115
FP_LNXG = 0
FP_LNXB = 512
FP_LNG = 1024
FP_LNB = 3072
FP_GMIX = 5120
FP_GOUT = 9216
NFP = 13312

ARENA_WORDS = 51200


def build_nc(T, debug=False, stop=None, single=False, dff=DFF):
    TQ = T if single else T // 4
    NT1 = T // 512
    NT2 = TQ // 256
    nc = bass.Bass("TRN2", target_bir_lowering=False)
    es = ExitStack()
    with es:
        P = Prog(nc, es)

        def dram(name, shape, dt, kind="ExternalInput"):
            return nc.dram_tensor(name, shape, dt, kind=kind).ap()
        xq = dram("xq", [TQ, D], F32)
        wrw = dram("wrw", [15, 128, KC * 128], F32)
        big = stop not in ('p0', 'p1', 'p1a', 'p1b', 'p1c', 'p1d')
        NU = {'wg': 32, 'wo': 32, 'w1': dff // 128, 'w2': dff // 128}
        if big and not single:
            wsrc = {nm: dram(nm, [NU[nm] // 4 * 64, 8192], F32) for nm in NU}
        pp_d = dram("pp", [128, NPP], F32)
        fp_d = dram("fp", [1, NFP], F32)
        lora_d = dram("lora_up", [128, 512], F32)
        gate_d = dram("gate_up", [256, 512], F32)
        ws_d = dram("ws", [16, 128, 128], F32)
        out_d = dram("out", [TQ, D], F32, kind="ExternalOutput")
        wrw_bf = dram("wrw_bf", [15, 128, KC * 128], BF16, kind="Internal")
        if big:
            if single:
                wbf = {nm: dram(nm + "_bf", [NU[nm] * 64, 8192], BF16) for nm in NU}
            else:
                wsh = {nm: dram(nm + "_sh", [NU[nm] // 4 * 64, 8192], BF16, kind="Internal") for nm in NU}
                wbf = {nm: dram(nm + "_bf", [NU[nm] * 64, 8192], BF16, kind="Internal") for nm in NU}
            wslabs = {nm: wbf[nm].rearrange("(s p) e -> s p e", p=128) for nm in NU}
        PR = min(1024, T)
        h_loc = dram("h_loc", [TQ, D], BF16, kind="Internal")
        h_full = h_loc if single else dram("h_full", [T, D], BF16, kind="Internal")
        yb_loc = dram("yb_loc", [T, 512], BF16, kind="Internal")
        yb_gat = dram("yb_gat", [4 * T, 512], BF16, kind="Internal")
        dbg = {}
        if debug:
            dbg['yb'] = dram("dbg_yb", [T, 512], F32, kind="ExternalOutput")
            dbg['ysc'] = dram("dbg_ysc", [T, 512], F32, kind="ExternalOutput")
            dbg['x1'] = dram("dbg_x1", [TQ, D], F32, kind="ExternalOutput")
            dbg['ya'] = dram("dbg_ya", [TQ, D], F32, kind="ExternalOutput")

        arena_t = es.enter_context(nc.sbuf_tensor("arena", [128, ARENA_WORDS], F32))
        PS = [es.enter_context(nc.psum_tensor("psb%d" % i, [128, 512], F32)) for i in range(8)]
        ccsem = es.enter_context(nc.semaphore("ccsem"))

        def V(name, r, w, **kw):
            P.op('dve', name, r, w, **kw)

        def G(name, r, w, **kw):
            P.op('pool', name, r, w, **kw)

        def ACT(out, in_, func, r, w, **kw):
            P.op('act', 'activation', r, w, out=out, in_=in_, func=func, **kw)

        def MM(out, lhsT, rhs, start, stop, r, w, inc=True):
            P.op('pe', 'matmul', r, w, inc=inc, out=out, lhsT=lhsT, rhs=rhs, start=start, stop=stop)

        def TR(out, in_, identity, r, w, inc=True):
            P.op('pe', 'transpose', r, w, inc=inc, out=out, in_=in_, identity=identity)

        F32R = mybir.dt.float32r

        def R32(ap):
            return ap.bitcast(F32R) if USE_F32R else ap

        def bc3(ap2, n):
            return ap2.unsqueeze(2).broadcast_to([128, ap2.shape[1], n])

        def cast(chan, dst, src, cols, key):
            if cols > 2048:
                s2 = src.rearrange("r (a c) -> (r a) c", c=2048)
                d2 = dst.rearrange("r (a c) -> (r a) c", c=2048)
            else:
                s2, d2 = src, dst
            nr = s2.shape[0]
            step = 8192
            keys = []
            for i, r0 in enumerate(range(0, nr, step)):
                r1 = min(nr, r0 + step)
                P.dma('pool', chan + str(i % 2), [], [key + ".%d" % i], out=d2[r0:r1, :], in_=s2[r0:r1, :])
                keys.append(key + ".%d" % i)
            return keys

        k_wrw = cast('c0', wrw_bf.rearrange("a p n -> (a p) n"), wrw.rearrange("a p n -> (a p) n"), 4096, 'wrw_bf')

        A = Arena(arena_t, ARENA_WORDS)
        identb = A.bf16(128)
        identf = A.f32(128)
        mask4 = A.f32(512)
        maskL = A.f32(128)
        bones = A.f32(128)
        Esel = A.f32(2)
        scanmask = A.f32(512)
        pp = A.f32(NPP)
        epsc = A.f32(4)
        lnxg = A.f32(512)
        lnxb = A.f32(512)
        lora_up = A.bf16(512)
        gate_up = A.bf16(2, 512)
        H2 = A.f32(4, 64)
        plast = A.f32(16)
        small = A.f32(64)
        xt_off = A.off
        xt = A.f32(4096)
        xn = A.bf16(4096)
        hT = A.bf16(KC, 512)
        wsl = [A.bf16(KC, 128) for _ in range(2)]
        praw = [A.f32(513) for _ in range(2)]
        tmpd = A.f32(512)
        pl = A.f32(512)
        pg = A.f32(2, 512)
        lt = A.bf16(512)
        sgT = A.bf16(2, 512)
        prkv = A.f32(3, 512)
        gtok = A.f32(4, 512)
        ytok = A.f32(4, 512)
        vtok = A.f32(4, 512)
        bon = A.f32(4, 8)
        sw = A.f32(512)
        cum = A.f32(512)
        cumx = A.f32(512)
        Gin = A.f32(512)
        Gex = A.f32(512)
        Ginv = A.f32(512)
        Grat = A.f32(512)
        GC = A.f32(4)
        aa = A.f32(512)
        kkn = A.f32(512)
        km = A.f32(512)
        t2 = A.f32(512)
        ar = A.f32(2, 512)
        bbar = A.f32(512)
        kbar = A.f32(512)
        btil = A.f32(512)
        ktil = A.f32(512)
        rk = A.f32(512)
        tokmS = [A.f32(4, 128) for _ in range(2)]
        zsbS = [A.f32(64) for _ in range(4)]
        tauS = [A.f32(128) for _ in range(4)]
        PTsS = [A.f32(64) for _ in range(4)]
        GTsS = [A.f32(128) for _ in range(4)]
        scS = [A.f32(512) for _ in range(2)]
        W3S = [[A.f32(384) for _ in range(2)] for _ in range(1)]
        fin = [A.f32(512) for _ in range(3 if debug else 2)]
        Ax = Arena(arena_t, ARENA_WORDS)
        Ax.off = xt_off
        scS += [Ax.f32(512) for _ in range(2)]
        W3S += [[Ax.f32(384) for _ in range(2)] for _ in range(3)]
        assert Ax.off <= xt_off + 4096
        UB = [(6, 7), (2, 3), (5, 1), (0, 4)]
        BK = ['psT0', 'psT1', 'ps2', 'ps3', 'ps4', 'ps5', 'ps6', 'ps7']
        ygb = A.bf16(512)
        st8 = A.f32(32)

        G('memset', [], ['const'], ap=identf, constant=1.0)
        G('affine_select', ['const'], ['const'], out=identf, in_=identf, pattern=[[-1, 128]], compare_op=ALU.is_equal,
          fill=0.0, base=0, channel_multiplier=1)
        G('tensor_copy', ['const'], ['const'], out=identb, in_=identf)
        G('memset', ['const'], ['const'], ap=mask4, constant=1.0)
        for i in range(4):
            G('affine_select', ['const'], ['const'], out=mask4[:, i * 128:(i + 1) * 128], in_=mask4[:, i * 128:(i + 1) * 128],
              pattern=[[1, 128]], compare_op=ALU.is_ge, fill=0.0, base=(-1 if i % 2 == 0 else 0), channel_multiplier=-1)
        G('memset', ['const'], ['const'], ap=maskL, constant=1.0)
        G('affine_select', ['const'], ['const'], out=maskL, in_=maskL, pattern=[[-1, 128]], compare_op=ALU.is_ge,
          fill=0.0, base=-1, channel_multiplier=1)
        G('memset', ['const'], ['const'], ap=bones, constant=0.0)
        G('memset', ['const'], ['const'], ap=bones[0:64, 0:64], constant=1.0)
        G('memset', ['const'], ['const'], ap=bones[64:128, 64:128], constant=1.0)
        G('memset', ['const'], ['const'], ap=Esel, constant=0.0)
        G('memset', ['const'], ['const'], ap=Esel[0:64, 0:1], constant=1.0)
        G('memset', ['const'], ['const'], ap=Esel[64:128, 1:2], constant=1.0)
        G('memset', ['const'], ['const'], ap=scanmask, constant=1.0)
        for c in range(4):
            G('memset', ['const'], ['const'], ap=scanmask[:, c * 128:c * 128 + 1], constant=0.0)
        G('memset', ['const'], ['const'], ap=epsc[:, 0:1], constant=1e-6)
        G('memset', ['const'], ['const'], ap=epsc[:, 1:2], constant=64e-5)
        G('memset', [], ['H%d' % h for h in range(8)], ap=H2, constant=0.0)
        G('memset', [], ['plast'], ap=plast, constant=0.0)
        P.dma('sp', 'pp', [], ['pp'], out=pp, in_=pp_d)
        P.dma('sp', 'lnxg', [], ['lnxg'], out=lnxg, in_=fp_d[:, FP_LNXG:FP_LNXG + 512].partition_broadcast(128))
        P.dma('sp', 'lnxb', [], ['lnxb'], out=lnxb, in_=fp_d[:, FP_LNXB:FP_LNXB + 512].partition_broadcast(128))
        P.dma('pool', 'lora', [], ['lora_up'], out=lora_up, in_=lora_d)
        P.dma('pool', 'gateup', [], ['gate_up'], out=gate_up, in_=gate_d.rearrange("(j p) n -> p j n", p=128))

        ncc = [0]

        def collective(groups, src, dst, rkeys, wkeys):
            P.wait_all('pool', rkeys)
            ncc[0] += 1
            P.q['pool'].append(lambda e, src=src, dst=dst, groups=groups: e.collective_compute(
                "AllGather", ALU.bypass, replica_groups=groups, ins=[src], outs=[dst]).then_inc(ccsem, 1))
            n = ncc[0]
            return n

        def wait_cc(n, engines=None):
            for en in (engines or P.ENG):
                P.q[en].append(lambda e, n=n: e.wait_ge(ccsem, n))

        if stop == 'p0':
            P.barrier()
            P.replay()
            return nc
        psT_b = [PS[0].bitcast(BF16), PS[1].bitcast(BF16)]

        def rms_rows(src, dst, ksrc, kdst, sm, epsap, kc_):
            ss, rs = sm[:, 0:1], sm[:, 1:2]
            ACT(dst, src, AF.Square, [ksrc], [kdst, 'ss'], accum_out=ss)
            ACT(rs, ss, AF.Sqrt, ['ss', kc_], ['rs'], scale=1.0 / D, bias=epsap)
            V('reciprocal', ['rs'], ['rs'], out=rs, in_=rs)
            V('tensor_scalar', [ksrc, 'rs'], [kdst], out=dst, in0=src, scalar1=rs, scalar2=None, op0=ALU.mult)

        def transposes(src, dstT, col0, gains, ksrc, kdst, idb, kconst, kps):
            for b4 in range(4):
                bank = b4 % 2
                for i in range(8):
                    kc = b4 * 8 + i
                    TR(psT_b[bank][:, i * 128:(i + 1) * 128], src[:, kc * 128:(kc + 1) * 128], idb,
                       [ksrc, kconst], [kps % bank], inc=(i == 7))
                dst = dstT[:, b4 * 8:(b4 + 1) * 8, col0:col0 + 128]
                src3 = psT_b[bank][:, :].rearrange("p (a b) -> p a b", a=8, b=128)
                if gains is None:
                    ACT(dst, src3, AF.Copy, [kps % bank], [kdst])
                else:
                    V('tensor_tensor', [kps % bank, gains[1]], [kdst], out=dst, in0=src3,
                      in1=bc3(gains[0][:, b4 * 8:(b4 + 1) * 8], 128), op=ALU.mult)

        slab_i = [0]

        def project(ch, dst, kdst):
            si = slab_i[0] % 2
            slab_i[0] += 1
            ks = 'wsl%d' % si
            P.dma('sp', ks, k_wrw, [ks], out=wsl[si], in_=wrw_bf[ch].rearrange("p (k n) -> p k n", k=KC))
            bank = 2 + si
            kp = 'ps%d' % bank
            for kc in range(KC):
                MM(PS[bank][:, :], wsl[si][:, kc, :], hT[:, kc, :], kc == 0, kc == KC - 1, [ks, 'hT'], [kp], inc=(kc == KC - 1))
            pr = praw[si]
            kr = 'praw%d' % si
            ACT(pr[:, 1:513], PS[bank][:, :], AF.Copy, [kp], [kr])
            G('tensor_copy', ['plast', kr], [kr], out=pr[:, 0:1], in_=plast[:, ch:ch + 1])
            V('tensor_tensor', [kr], ['tmpd'], out=tmpd, in0=pr[:, 0:512], in1=pr[:, 1:513], op=ALU.subtract)
            V('scalar_tensor_tensor', ['tmpd', 'pp', kr], [kdst], out=dst, in0=tmpd, scalar=pp[:, PP_MU + ch:PP_MU + ch + 1],
              in1=pr[:, 1:513], op0=ALU.mult, op1=ALU.add)
            G('tensor_copy', [kr, 'plast'], ['plast'], out=plast[:, ch:ch + 1], in_=pr[:, 512:513])

        for st_ in range(TQ // 128):
            P.dma('sp', 'xt', [], ['xt'], out=xt, in_=xq[st_ * 128:(st_ + 1) * 128, :])
            rms_rows(xt, xn, 'xt', 'xn', small, epsc[:, 0:1], 'const')
            P.dma('sp', 'hloc', ['xn'], ['h_loc%d' % st_], out=h_loc[st_ * 128:(st_ + 1) * 128, :], in_=xn)
        cc_h0 = ncc[0]
        if not single:
            for i in range(TQ // 128):
                cc_h = collective([[0, 1, 2, 3], [4, 5, 6, 7]], h_loc[i * 128:(i + 1) * 128, :], h_full[i * 512:(i + 1) * 512, :],
                                  ['h_loc%d' % i], ['h_full'])
        cc_w = {}
        pending = []
        if big and not single:
            stg1 = A.bf16(4096)
            pending = [(nm, j) for nm in ('wg', 'wo', 'w1', 'w2') for j in range(NU[nm] // 4)]
        per_tile = (len(pending) + NT1 - 1) // NT1 if NT1 else 0

        def do_bounces(n):
            for _ in range(min(n, len(pending))):
                nm, j = pending.pop(0)
                src = wsrc[nm][j * 64:(j + 1) * 64, :].rearrange("r (h f) -> (r h) f", h=2)
                dst = wsh[nm][j * 64:(j + 1) * 64, :].rearrange("r (h f) -> (r h) f", h=2)
                P.dma('pool', 'stg', [], ['stg'], out=stg1.rearrange("p (a c) -> p a c", a=2), in_=src.rearrange("p (a c) -> p a c", a=2))
                P.dma('sp', 'stgo', ['stg'], [nm + '_sh%d' % j], out=dst, in_=stg1)
                cc_w[nm] = collective([[0, 1, 2, 3], [4, 5, 6, 7]], wsh[nm][j * 64:(j + 1) * 64, :], wbf[nm][j * 256:(j + 1) * 256, :],
                                      [nm + '_sh%d' % j], [nm + '_bf'])
        if not single and stop in ('pA', 'pB'):
            wait_cc(ncc[0])
        if stop in ('pA', 'pB'):
            P.barrier()
            wait_cc(ncc[0])
            P.replay()
            return nc

        class _Stop(Exception):
            pass
        ydone = set()

        def pending_block():
            return stop in ('p1', 'p1a', 'p1b', 'p1c', 'p1d', 'p1big')
        P.alias(['sc2', 'sc3'] + ['W3_%d_%d' % (u, k) for u in (1, 2, 3) for k in (0, 1)], ['xt'])

        def phase1():
          for tt in range(NT1):
              t0 = tt * 512
              if stop == 'p1a' and tt == 1:
                  raise _Stop()
              do_bounces(per_tile)
              for s in range(4):
                  st_ = (t0 // 128) + s
                  hrow = st_ * 128 if single else (st_ % (TQ // 128)) * 512 + (st_ // (TQ // 128)) * 128
                  if not single and tt * 4 + s < TQ // 128:
                      wait_cc(cc_h0 + (st_ % (TQ // 128)) + 1, ['sp'])
                  P.dma('sp', 'xn', [], ['xn'], out=xn, in_=h_full[hrow:hrow + 128, :])
                  transposes(xn, hT, s * 128, (pp[:, PP_GPRE:PP_GPRE + 32], 'pp'), 'xn', 'hT', identb, 'const', 'psT%d')
              if stop == 'p1a':
                  continue

              project(12, pl, 'pl')
              project(13, pg[:, 0, :], 'pg0')
              project(14, pg[:, 1, :], 'pg1')
              ACT(lt[0:64, :], pl[0:64, :], AF.Tanh, ['pl'], ['lt'])
              ACT(lt[64:128, :], pl[64:128, :], AF.Copy, ['pl', 'lt'], ['lt'])
              ACT(sgT, pg, AF.Sigmoid, ['pg0', 'pg1'], ['sgT'])
              for c in range(4):
                  bank = 2 + (c % 2)
                  for jc in range(2):
                      MM(PS[bank][:, :], sgT[:, jc, c * 128:(c + 1) * 128], gate_up[:, jc, :], jc == 0, jc == 1,
                         ['sgT', 'gate_up'], ['ps%d' % bank], inc=(jc == 1))
                  ACT(gtok[:, c, :], PS[bank][:, :], AF.Copy, ['ps%d' % bank], ['gtok%d' % c])
              if stop == 'p1b' and c == 3:
                  raise _Stop()

              for j in range(4):
                  project(j, prkv[:, 0, :], 'pr')
                  project(4 + j, prkv[:, 1, :], 'pk')
                  project(8 + j, prkv[:, 2, :], 'pv')
                  rs_, ks_, vs_ = prkv[:, 0, :], prkv[:, 1, :], prkv[:, 2, :]

                  def col(base, j=j):
                      return pp[:, base + j:base + j + 1]
                  MM(PS[4][:, :], lora_up[0:64, j * 128:(j + 1) * 128], lt[0:64, :], True, True, ['lora_up', 'lt'], ['ps4'])
                  ACT(sw, PS[4][:, :], AF.Sigmoid, ['ps4', 'pp'], ['sw'], bias=col(PP_W0))
                  G('tensor_scalar', ['sw'], ['sw'], out=sw, in0=sw, scalar1=-0.6065306597126334, scalar2=None, op0=ALU.mult)
                  V('tensor_tensor_scan', ['sw', 'const'], ['cum'], out=cum, data0=scanmask, data1=sw, initial=0.0,
                    op0=ALU.mult, op1=ALU.add)
                  G('tensor_tensor', ['cum', 'sw'], ['cumx'], out=cumx, in0=cum, in1=sw, op=ALU.subtract)
                  ACT(Gin, cum, AF.Exp, ['cum'], ['Gin'])
                  ACT(Gex, cumx, AF.Exp, ['cumx'], ['Gex'])
                  ACT(Ginv, cum, AF.Exp, ['cum'], ['Ginv'], scale=-1.0)
                  cumC = cum[:, 127::128]
                  V('tensor_tensor', ['cum'], ['Grat'], out=Grat.rearrange("p (c t) -> p c t", c=4), in0=bc3(cumC, 128),
                    in1=cum.rearrange("p (c t) -> p c t", c=4), op=ALU.subtract)
                  ACT(Grat, Grat, AF.Exp, ['Grat'], ['Grat'])
                  ACT(GC, cumC, AF.Exp, ['cum'], ['GC'])
                  MM(PS[4][:, :], lora_up[64:128, j * 128:(j + 1) * 128], lt[64:128, :], True, True, ['lora_up', 'lt'], ['ps4'])
                  ACT(aa, PS[4][:, :], AF.Sigmoid, ['ps4', 'pp'], ['aa'], bias=col(PP_A0))
                  G('tensor_scalar', ['pk', 'pp'], ['kkn'], out=kkn, in0=ks_, scalar1=col(PP_KK), scalar2=None, op0=ALU.mult)
                  G('tensor_tensor', ['kkn'], ['t2'], out=t2, in0=kkn, in1=kkn, op=ALU.mult)
                  MM(PS[4][:, :], bones, t2, True, True, ['const', 't2'], ['ps4'])
                  ACT(t2, PS[4][:, :], AF.Sqrt, ['ps4'], ['t2'])
                  V('tensor_scalar', ['t2'], ['t2'], out=t2, in0=t2, scalar1=1e-12, scalar2=None, op0=ALU.max)
                  V('reciprocal', ['t2'], ['t2'], out=t2, in_=t2)
                  V('tensor_tensor', ['kkn', 't2'], ['kkn'], out=kkn, in0=kkn, in1=t2, op=ALU.mult)
                  V('tensor_scalar', ['aa', 'pp'], ['km'], out=km, in0=aa, scalar1=col(PP_KA), scalar2=col(PP_KA),
                    op0=ALU.mult, op1=ALU.subtract)
                  V('scalar_tensor_tensor', ['km', 'pk'], ['km'], out=km, in0=km, scalar=1.0, in1=ks_, op0=ALU.add, op1=ALU.mult)
                  V('scalar_tensor_tensor', ['kkn', 'Gex'], ['ar0'], out=ar[:, 0, :], in0=kkn, scalar=-1.0, in1=Gex,
                    op0=ALU.mult, op1=ALU.mult)
                  G('tensor_tensor', ['pr', 'Gin'], ['ar1'], out=ar[:, 1, :], in0=rs_, in1=Gin, op=ALU.mult)
                  G('tensor_tensor', ['kkn', 'aa', 't2'], ['t2'], out=t2, in0=kkn, in1=aa, op=ALU.mult)
                  V('tensor_tensor', ['t2', 'Ginv'], ['bbar'], out=bbar, in0=t2, in1=Ginv, op=ALU.mult)
                  G('tensor_tensor', ['km', 'Ginv'], ['kbar'], out=kbar, in0=km, in1=Ginv, op=ALU.mult)
                  V('tensor_tensor', ['t2', 'Grat'], ['btil'], out=btil, in0=t2, in1=Grat, op=ALU.mult)
                  G('tensor_tensor', ['km', 'Grat'], ['ktil'], out=ktil, in0=km, in1=Grat, op=ALU.mult)
                  V('scalar_tensor_tensor', ['pr', 'pp', 'km'], ['rk'], out=rk, in0=rs_, scalar=col(PP_RK), in1=km,
                    op0=ALU.mult, op1=ALU.mult)

                  KCH = 2
                  for cg0 in range(0, 4, KCH):
                      for cc in range(KCH):
                          c = cg0 + cc
                          cs = slice(c * 128, (c + 1) * 128)
                          tk_ = 'tokm%d' % cc
                          MM(PS[4][:, 0:2], rk[:, cs], Esel, True, True, ['rk', 'const'], ['ps4'])
                          ACT(bon[:, c, 2 * j:2 * j + 2], PS[4][:, 0:2], AF.Copy, ['ps4'], ['bon%d' % c])
                          srcs = [(ar[:, 0, cs], 'ar0'), (vs_[:, cs], 'pv'), (btil[:, cs], 'btil'), (ktil[:, cs], 'ktil')]
                          for i, (sap, skey) in enumerate(srcs):
                              TR(PS[4][:, i * 128:(i + 1) * 128], sap, identf, [skey, 'const'], ['ps4'], inc=(i == 3))
                          ACT(tokmS[cc], PS[4][:, :].rearrange("p (a b) -> p a b", a=4, b=128), AF.Copy, ['ps4'], [tk_])
                          G('tensor_copy', [tk_], ['vtok%d' % c], out=vtok[:, c, j * 128:(j + 1) * 128], in_=tokmS[cc][:, 1, :])

                      def partA(u, cc, i, j=j, cg0=cg0):
                          c = cg0 + cc
                          cs = slice(c * 128, (c + 1) * 128)
                          pb = slice(64 * i, 64 * i + 64)
                          tk_ = 'tokm%d' % cc
                          Atok, Vtok, Btok, Ktok = (tokmS[cc][:, n, pb] for n in range(4))
                          PA, PB = PS[UB[u][0]], PS[UB[u][1]]
                          ka, kb = BK[UB[u][0]], BK[UB[u][1]]
                          sc_, W3_ = scS[u], W3S[u]
                          ksc = 'sc%d' % u
                          MM(PA[:, 0:256], bbar[pb, cs], ar[pb, :, cs], True, True, ['bbar', 'ar0', 'ar1'], [ka], inc=False)
                          MM(PA[:, 256:512], kbar[pb, cs], ar[pb, :, cs], True, True, ['kbar', 'ar0', 'ar1'], [ka])
                          MM(PB[:, 384:512], ar[pb, 0, cs], bbar[pb, cs], True, True, ['bbar', 'ar0'], [kb])
                          V('tensor_tensor', [ka, 'const'], [ksc], out=sc_, in0=PA[:, :], in1=mask4, op=ALU.mult)
                          k0, k1 = 'W3_%d_0' % u, 'W3_%d_1' % u
                          V('tensor_tensor', [kb, 'const'], [k0], out=W3_[0][:, 0:128], in0=PB[:, 384:512], in1=maskL, op=ALU.mult)
                          G('tensor_copy', [ksc, k0], [k0], out=W3_[0][:, 128:256], in_=sc_[:, 0:128])
                          G('tensor_copy', ['const', k0], [k0], out=W3_[0][:, 256:384], in_=identf)
                          yield
                          for rnd in range(7):
                              wi, wo_ = W3_[rnd % 2], W3_[(rnd + 1) % 2]
                              ki, ko = (k0, k1) if rnd % 2 == 0 else (k1, k0)
                              if rnd < 6:
                                  MM(PB[:, 128:384], R32(wi[:, 0:128]), R32(wi[:, 128:384]), True, True, [ki], [kb], inc=False)
                                  MM(PB[:, 0:128], R32(wi[:, 128:256]), R32(wi[:, 0:128]), True, True, [ki], [kb])
                                  ACT(wo_[:, 0:256], PB[:, 0:256], AF.Copy, [kb], [ko])
                              else:
                                  MM(PB[:, 256:384], R32(wi[:, 0:128]), R32(wi[:, 256:384]), True, True, [ki], [kb])
                              V('tensor_tensor', [kb, ki, ko], [ko], out=wo_[:, 256:384], in0=PB[:, 256:384],
                                in1=wi[:, 256:384], op=ALU.add)
                              yield
                          X = W3_[1][:, 256:384]
                          kX = k1
                          Mak, Mrb = sc_[:, 256:384], sc_[:, 128:256]
                          MM(PA[:, 0:64], Mak, Vtok, True, True, [ksc, tk_], [ka])
                          ACT(zsbS[u], PA[:, 0:64], AF.Copy, [ka], ['zsb%d' % u])
                          yield
                          MM(PA[:, 64:128], X, Atok, True, True, [kX, tk_], [ka], inc=False)
                          MM(PA[:, 128:192], X, zsbS[u], True, True, [kX, 'zsb%d' % u], [ka])
                          V('tensor_copy', [ka], ['tau%d' % u], out=tauS[u], in_=PA[:, 64:192])
                          yield
                          MM(PA[pb, 192:256], tauS[u][:, 0:64], Btok, True, True, ['tau%d' % u, tk_], [ka], inc=False)
                          MM(PA[pb, 256:384], tauS[u][:, 0:64], Mrb, True, True, ['tau%d' % u, ksc], [ka])
                          ACT(PTsS[u][pb, :], PA[pb, 192:256], AF.Copy, [ka], ['PTs%d' % u])
                          V('tensor_tensor', [ka, 'ar1'], ['GTs%d' % u], out=GTsS[u][pb, :], in0=PA[pb, 256:384], in1=ar[pb, 1, cs], op=ALU.add)
                          yield

                      def partB(u, cc, i, j=j, cg0=cg0):
                          c = cg0 + cc
                          hh = 2 * j + i
                          Hk = 'H%d' % hh
                          pb = slice(64 * i, 64 * i + 64)
                          tk_ = 'tokm%d' % cc
                          Atok, Vtok, Btok, Ktok = (tokmS[cc][:, n, pb] for n in range(4))
                          PA, PB = PS[UB[u][0]], PS[UB[u][1]]
                          ka, kb = BK[UB[u][0]], BK[UB[u][1]]
                          sc_ = scS[u]
                          ksc = 'sc%d' % u
                          Mrb, Mrk = sc_[:, 128:256], sc_[:, 384:512]
                          MM(PA[:, 384:448], Mrb, tauS[u][:, 64:128], True, False, [ksc, 'tau%d' % u], [ka], inc=False)
                          MM(PA[:, 384:448], Mrk, Vtok, False, False, [ksc, tk_], [ka], inc=False)
                          MM(PA[:, 384:448], GTsS[u][pb, :], H2[pb, j, :], False, True, ['GTs%d' % u, Hk], [ka])
                          ACT(ytok[:, c, hh * 64:(hh + 1) * 64], PA[:, 384:448], AF.Copy, [ka], ['ytok%d' % c])
                          MM(PB[pb, 0:64], Btok, tauS[u][:, 64:128], True, False, [tk_, 'tau%d' % u], [kb], inc=False)
                          MM(PB[pb, 0:64], Ktok, Vtok, False, False, [tk_], [kb], inc=False)
                          MM(PB[pb, 0:64], PTsS[u][pb, :], H2[pb, j, :], False, True, ['PTs%d' % u, Hk], [kb])
                          V('scalar_tensor_tensor', [Hk, 'GC', kb], [Hk], out=H2[pb, j, :], in0=H2[pb, j, :],
                            scalar=GC[pb, c:c + 1], in1=PB[pb, 0:64], op0=ALU.mult, op1=ALU.add)

                      gens = [partA(cc * 2 + i, cc, i) for cc in range(KCH) for i in range(2)]
                      while gens:
                          for g_ in list(gens):
                              try:
                                  next(g_)
                              except StopIteration:
                                  gens.remove(g_)
                      for cc in range(KCH):
                          for i in range(2):
                              partB(cc * 2 + i, cc, i)
                      if stop == 'p1d':
                          raise _Stop()

              for c in range(4):
                  y3 = ytok[:, c, :].rearrange("p (h n) -> p h n", h=8)
                  f0, f1 = fin[0], fin[1]
                  f2 = fin[2] if debug else None
                  f03 = f0.rearrange("p (h n) -> p h n", h=8)
                  f13 = f1.rearrange("p (h n) -> p h n", h=8)
                  mu8, var8, rs8 = st8[:, 0:8], st8[:, 8:16], st8[:, 16:24]
                  yk = 'ytok%d' % c
                  r0 = t0 + c * 128
                  if debug:
                      P.dma('sp', 'dbg1', [yk], ['dbg_ysc'], out=dbg['ysc'][r0:r0 + 128, :], in_=ytok[:, c, :])
                  V('tensor_reduce', [yk], ['mu8'], out=mu8, in_=y3, axis=AX.X, op=ALU.add)
                  V('tensor_scalar', ['mu8'], ['mu8'], out=mu8, in0=mu8, scalar1=1.0 / 64, scalar2=None, op0=ALU.mult)
                  V('tensor_tensor', [yk, 'mu8'], ['f0'], out=f03, in0=y3, in1=bc3(mu8, 64), op=ALU.subtract)
                  G('tensor_tensor', ['f0'], ['f1'], out=f1, in0=f0, in1=f0, op=ALU.mult)
                  V('tensor_reduce', ['f1'], ['var8'], out=var8, in_=f13, axis=AX.X, op=ALU.add)
                  ACT(rs8, var8, AF.Sqrt, ['var8', 'const'], ['rs8'], scale=1.0 / 64, bias=epsc[:, 1:2])
                  V('reciprocal', ['rs8'], ['rs8'], out=rs8, in_=rs8)
                  V('tensor_tensor', ['f0', 'rs8'], ['f0'], out=f03, in0=f03, in1=bc3(rs8, 64), op=ALU.mult)
                  G('tensor_tensor', ['f0', 'lnxg'], ['f0'], out=f0, in0=f0, in1=lnxg, op=ALU.mult)
                  G('tensor_tensor', ['f0', 'lnxb'], ['f0'], out=f0, in0=f0, in1=lnxb, op=ALU.add)
                  v3 = vtok[:, c, :].rearrange("p (h n) -> p h n", h=8)
                  V('tensor_tensor', ['vtok%d' % c, 'bon%d' % c, 'f1'], ['f1'], out=f13, in0=v3, in1=bc3(bon[:, c, :], 64), op=ALU.mult)
                  G('tensor_tensor', ['f0', 'f1'], ['f0'], out=f0, in0=f0, in1=f1, op=ALU.add)
                  V('tensor_tensor', ['f0', 'gtok%d' % c], ['ygb'], out=ygb, in0=f0, in1=gtok[:, c, :], op=ALU.mult)
                  P.dma('sp', 'ygb', ['ygb'], ['yb_loc%d' % (r0 // PR)], out=yb_loc[r0:r0 + 128, :], in_=ygb)
                  if debug:
                      V('tensor_tensor', ['f0', 'gtok%d' % c], ['f2'], out=f2, in0=f0, in1=gtok[:, c, :], op=ALU.mult)
                      P.dma('sp', 'dbg2', ['f2'], ['dbg_yb'], out=dbg['yb'][r0:r0 + 128, :], in_=f2)
              if not single and (t0 + 512) % PR == 0 and not pending_block():
                  ip = (t0 + 512) // PR - 1
                  collective([[0, 1, 2, 3], [4, 5, 6, 7]], yb_loc[ip * PR:(ip + 1) * PR, :], yb_gat[ip * 4 * PR:(ip + 1) * 4 * PR, :],
                             ['yb_loc%d' % ip], ['yb_gat'])
                  ydone.add(ip)

        try:
            phase1()
        except _Stop:
            pass
        do_bounces(len(pending))

        P.barrier()
        if stop in ('p1', 'p1a', 'p1b', 'p1c', 'p1d', 'p1big'):
            P.replay()
            return nc
        for i in range(0 if single else T // PR):
            if i in ydone:
                continue
            cc_y = collective([[0, 1, 2, 3], [4, 5, 6, 7]], yb_loc[i * PR:(i + 1) * PR, :], yb_gat[i * 4 * PR:(i + 1) * 4 * PR, :],
                              ['yb_loc%d' % i], ['yb_gat'])
        cc_y = ncc[0]
        if not single:
            wait_cc(cc_y)

        if stop == 'cc':
            P.replay()
            return nc
        A2 = Arena(arena_t, ARENA_WORDS)
        identb2 = A2.bf16(128)
        identf2 = A2.f32(128)
        pp2 = A2.f32(NPP)
        epsc2 = A2.f32(4)
        small2 = A2.f32(64)
        wsT = A2.bf16(16, 128)
        xres = A2.f32(2, 4096)
        ra0 = A2.off
        hT2 = A2.bf16(KC, 256)
        tmb = [A2.bf16(4096) for _ in range(2)]
        assert A2.off - ra0 == 8192
        rb0 = A2.off
        f1T = A2.bf16(128, 256)
        rb_end = A2.off
        slab = [A2.bf16(16, 512) for _ in range(3)]
        gbc = A2.f32(4096)
        rtmp = [A2.f32(256) for _ in range(2)]
        Aa = Arena(arena_t, ARENA_WORDS)
        Aa.off = ra0
        TMg = [Aa.f32(4096) for _ in range(2)]
        Ab = Arena(arena_t, ARENA_WORDS)
        Ab.off = rb0
        TMc = [Ab.f32(4096) for _ in range(2)]
        lng = Ab.f32(2048)
        lnb = Ab.f32(2048)
        vln = Ab.bf16(2048)
        cand = [Ab.bf16(4, 512) for _ in range(2)]
        wstmp = Ab.f32(128)
        assert Ab.off <= rb_end, (Ab.off, rb_end)

        G('memset', [], ['c2'], ap=identf2, constant=1.0)
        G('affine_select', ['c2'], ['c2'], out=identf2, in_=identf2, pattern=[[-1, 128]], compare_op=ALU.is_equal,
          fill=0.0, base=0, channel_multiplier=1)
        G('tensor_copy', ['c2'], ['c2'], out=identb2, in_=identf2)
        G('memset', ['c2'], ['c2'], ap=epsc2[:, 0:1], constant=1e-6)
        G('memset', ['c2'], ['c2'], ap=epsc2[:, 2:3], constant=1e-5)
        P.dma('sp', 'pp', [], ['pp2'], out=pp2, in_=pp_d)
        for h in range(16):
            P.dma('sp', 'wstmp', [], ['wstmp'], out=wstmp, in_=ws_d[h])
            TR(PS[0][:, 0:128], wstmp, identf2, ['wstmp', 'c2'], ['q0'])
            ACT(wsT[:, h, :], PS[0][:, 0:128], AF.Copy, ['q0'], ['wsT'])
            G('memset', ['wsT'], ['wsT'], ap=wsT[64:128, h, 0:64], constant=0.0)

        slab_n = [0]

        def tokmajor_matmul(wname, wkeys, lhsT_of, nk16, lhs_keys, evac):
            for cg in range(8):
                banks = (2 + 2 * (cg % 2), 3 + 2 * (cg % 2))
                for ks in range(nk16):
                    si = slab_n[0] % 3
                    slab_n[0] += 1
                    kk_ = 'slab%d' % si
                    P.dma('sp', kk_, wkeys, [kk_], out=slab[si].rearrange("p a b -> p (a b)"), in_=wslabs[wname][cg * nk16 + ks])
                    for s in range(2):
                        for kc in range(16):
                            MM(PS[banks[s]][:, :], lhsT_of(ks * 16 + kc, s), slab[si][:, kc, :],
                               ks == 0 and kc == 0, ks == nk16 - 1 and kc == 15,
                               [kk_] + lhs_keys, ['q%d' % banks[s]], inc=(kc == 15))
                for s in range(2):
                    evac(cg, s, PS[banks[s]], 'q%d' % banks[s])

        def evac_sq(TM, ktm):
            def ev(cg, s, ps, pk):
                V('tensor_copy', [pk], [ktm % s], out=TM[s][:, cg * 512:(cg + 1) * 512], in_=ps[:, :])
                for hf in range(2):
                    a0 = 8 + s * 16 + cg * 2 + hf
                    ACT(rtmp[0], TM[s][:, cg * 512 + hf * 256:cg * 512 + (hf + 1) * 256], AF.Square, [ktm % s, 'rt0'], ['rt0', 'ssq%d' % s],
                        accum_out=small2[:, a0:a0 + 1])
            return ev

        def post_norm_residual(TM, tmk, s):
            sq, rs = small2[:, 5:6], small2[:, 6:7]
            V('tensor_reduce', ['ssq%d' % s], ['sq5'], out=sq, in_=small2[:, 8 + s * 16:24 + s * 16], axis=AX.X, op=ALU.add)
            ACT(rs, sq, AF.Sqrt, ['sq5', 'c2'], ['rs6'], scale=1.0 / D, bias=epsc2[:, 0:1])
            V('reciprocal', ['rs6'], ['rs6'], out=rs, in_=rs)
            V('scalar_tensor_tensor', [tmk, 'rs6', 'gbc'], [tmk], out=TM, in0=TM, scalar=rs, in1=gbc, op0=ALU.mult, op1=ALU.mult)
            G('tensor_tensor', [tmk, 'xres%d' % s], ['xres%d' % s], out=xres[:, s, :], in0=xres[:, s, :], in1=TM, op=ALU.add)

        hT2_of = lambda kc, s: hT2[:, kc, s * 128:(s + 1) * 128]
        sel_d = dram("sel", [128, 4], F32)
        sel = A2.f32(4)
        P.dma('sp', 'sel', [], ['sel'], out=sel, in_=sel_d)

        def p2stop(name):
            if stop == name:
                raise _Stop()

        def phase2():
          for tt in range(NT2):
              R0 = tt * 256
              p2stop('p2a')
              P.alias(['TMc0', 'TMc1', 'lng', 'lnb', 'vln', 'cand0', 'cand1'], ['f1T'])
              P.alias(['hT2', 'tmb0', 'tmb1'], ['TMg0', 'TMg1'])
              for s in range(2):
                  P.dma('sp', 'xres%d' % s, [], ['xres%d' % s], out=xres[:, s, :], in_=xq[R0 + s * 128:R0 + (s + 1) * 128, :])
              P.dma('sp', 'lng', [], ['lng'], out=lng, in_=fp_d[:, FP_LNG:FP_LNG + 2048].partition_broadcast(128))
              P.dma('sp', 'lnb', [], ['lnb'], out=lnb, in_=fp_d[:, FP_LNB:FP_LNB + 2048].partition_broadcast(128))
              P.dma('sp', 'gbc', [], ['gbc'], out=gbc, in_=fp_d[:, FP_GMIX:FP_GMIX + 4096].partition_broadcast(128))
              for s in range(2):
                  rms_rows(xres[:, s, :], tmb[s], 'xres%d' % s, 'tmb%d' % s, small2, epsc2[:, 0:1], 'c2')
                  transposes(tmb[s], hT2, s * 128, (pp2[:, PP_GPRE:PP_GPRE + 32], 'pp2'), 'tmb%d' % s, 'hT2', identb2, 'c2', 'q%d')

              def evacC(cg, s, ps, pk):
                  ACT(TMc[s][:, cg * 512:(cg + 1) * 512], ps[:, :], AF.Gelu_apprx_tanh, [pk], ['TMc%d' % s])
              p2stop('p2b')
              tokmajor_matmul('wg', [], hT2_of, 2, ['hT2'], evacC)
              p2stop('p2c')

              for s in range(2):
                  z = TMc[s]
                  zk = 'TMc%d' % s
                  vv = z[:, 2048:4096]
                  sm, sq, rs = small2[:, 2:3], small2[:, 3:4], small2[:, 4:5]
                  V('tensor_reduce', [zk], ['sm'], out=sm, in_=vv, axis=AX.X, op=ALU.add)
                  V('tensor_scalar', ['sm'], ['sm'], out=sm, in0=sm, scalar1=1.0 / 2048, scalar2=None, op0=ALU.mult)
                  V('tensor_scalar', [zk, 'sm'], [zk], out=vv, in0=vv, scalar1=sm, scalar2=None, op0=ALU.subtract)
                  ACT(vln, vv, AF.Square, [zk], ['vln', 'sq'], accum_out=sq)
                  ACT(rs, sq, AF.Sqrt, ['sq', 'c2'], ['rsd'], scale=1.0 / 2048, bias=epsc2[:, 2:3])
                  V('reciprocal', ['rsd'], ['rsd'], out=rs, in_=rs)
                  V('scalar_tensor_tensor', [zk, 'rsd', 'lng'], [zk], out=vv, in0=vv, scalar=rs, in1=lng, op0=ALU.mult, op1=ALU.mult)
                  G('tensor_tensor', [zk, 'lnb', 'vln'], ['vln'], out=vln, in0=vv, in1=lnb, op=ALU.add)
                  p2stop('p2d1')
                  for h4 in range(4):
                      bank = h4 % 2
                      for i in range(4):
                          h = h4 * 4 + i
                          MM(PS[bank][:, i * 128:(i + 1) * 128], wsT[:, h, :], vln[:, h * 128:(h + 1) * 128], True, True,
                             ['wsT', 'vln'], ['q%d' % bank], inc=(i == 3))
                      vsl = vv[:, h4 * 512:(h4 + 1) * 512].rearrange("p (a b) -> p a b", a=4, b=128)
                      V('tensor_tensor', ['q%d' % bank, 'pp2', zk], [zk], out=vsl,
                        in0=PS[bank][:, :].rearrange("p (a b) -> p a b", a=4, b=128),
                        in1=bc3(pp2[:, PP_BS + h4 * 4:PP_BS + h4 * 4 + 4], 128), op=ALU.add)
                  p2stop('p2d2')
                  tk = 'tmb%d' % s
                  G('tensor_tensor', [zk, tk], [tk], out=tmb[s][:, 0:2048], in0=z[:, 0:2048], in1=vv, op=ALU.mult)
                  if debug:
                      V('tensor_copy', [tk, zk], [zk], out=z[:, 0:2048], in_=tmb[s][:, 0:2048])
                  yb_dst = tmb[s][:, 2048:4096].rearrange("p (g c) -> p g c", g=4)
                  if single:
                      G('memset', [tk], [tk], ap=tmb[s][:, 2048:4096], constant=0.0)
                  for jq in range(0 if single else 4):
                      cb = cand[jq % 2]
                      ck = 'cand%d' % (jq % 2)
                      tg = jq * TQ + R0 + s * 128
                      pi_, rr = tg // PR, tg % PR
                      P.dma('sp', ck, [], [ck], out=cb,
                            in_=yb_gat[pi_ * 4 * PR:(pi_ + 1) * 4 * PR, :].rearrange("(g r) c -> r g c", g=4)[rr:rr + 128, :, :])
                      if jq == 0:
                          V('tensor_scalar', [ck, 'sel', tk], [tk], out=yb_dst, in0=cb, scalar1=sel[:, 0:1], scalar2=None, op0=ALU.mult)
                      else:
                          V('scalar_tensor_tensor', [ck, 'sel', tk], [tk], out=yb_dst, in0=cb, scalar=sel[:, jq:jq + 1], in1=yb_dst,
                            op0=ALU.mult, op1=ALU.add)
                  if debug:
                      V('tensor_copy', [tk, zk], [zk], out=z[:, 2048:4096], in_=tmb[s][:, 2048:4096])
                      P.dma('sp', 'dbg3', [zk], ['dbg_ya'], out=dbg['ya'][R0 + s * 128:R0 + (s + 1) * 128, :], in_=z)
                  p2stop('p2d3')
                  transposes(tmb[s], hT2, s * 128, None, tk, 'hT2', identb2, 'c2', 'q%d')

              p2stop('p2e')
              tokmajor_matmul('wo', [], hT2_of, 2, ['hT2'], evac_sq(TMc, 'TMc%d'))
              p2stop('p2f')
              p2stop('p2f_noact')
              for s in range(2):
                  post_norm_residual(TMc[s], 'TMc%d' % s, s)
                  if debug:
                      P.dma('sp', 'dbg4', ['xres%d' % s], ['dbg_x1'], out=dbg['x1'][R0 + s * 128:R0 + (s + 1) * 128, :], in_=xres[:, s, :])

              P.dma('sp', 'gbc', ['gbc'], ['gbc'], out=gbc, in_=fp_d[:, FP_GOUT:FP_GOUT + 4096].partition_broadcast(128))
              for s in range(2):
                  rms_rows(xres[:, s, :], tmb[s], 'xres%d' % s, 'tmb%d' % s, small2, epsc2[:, 0:1], 'c2')
                  transposes(tmb[s], hT2, s * 128, (pp2[:, PP_GFFN:PP_GFFN + 32], 'pp2'), 'tmb%d' % s, 'hT2', identb2, 'c2', 'q%d')
              P.alias(['f1T'], ['TMc0', 'TMc1', 'lng', 'lnb', 'vln', 'cand0', 'cand1'])
              for fg in range(dff // 256):
                  si = slab_n[0] % 3
                  slab_n[0] += 1
                  kk_ = 'slab%d' % si
                  sl = slab[si].rearrange("p a b -> p (a b)").rearrange("p (k n) -> p k n", k=KC, n=256)
                  P.dma('sp', kk_, [], [kk_], out=slab[si].rearrange("p a b -> p (a b)"), in_=wslabs['w1'][fg])
                  bank = 2 + (fg % 4)
                  for fi in range(2):
                      for kc in range(KC):
                          MM(PS[bank][:, fi * 256:(fi + 1) * 256], sl[:, kc, fi * 128:(fi + 1) * 128], hT2[:, kc, :],
                             kc == 0, kc == KC - 1, [kk_, 'hT2'], ['q%d' % bank], inc=(kc == KC - 1))
                  for fi in range(2):
                      rt = rtmp[fi]
                      ACT(rt, PS[bank][:, fi * 256:(fi + 1) * 256], AF.Relu, ['q%d' % bank], ['rt%d' % fi])
                      G('tensor_tensor', ['rt%d' % fi], ['f1T'], out=f1T[:, fg * 2 + fi, :], in0=rt, in1=rt, op=ALU.mult)
              P.alias(['TMg0', 'TMg1'], ['hT2', 'tmb0', 'tmb1'])
              tokmajor_matmul('w2', [], lambda kc, s: f1T[:, kc, s * 128:(s + 1) * 128], dff // 2048, ['f1T'], evac_sq(TMg, 'TMg%d'))
              for s in range(2):
                  post_norm_residual(TMg[s], 'TMg%d' % s, s)
                  P.dma('sp', 'out%d' % s, ['xres%d' % s], ['out%d' % s], out=out_d[R0 + s * 128:R0 + (s + 1) * 128, :], in_=xres[:, s, :])
        try:
            phase2()
        except _Stop:
            P.barrier()
            P.replay()
            return nc
        P.wait_all('sp', ['out0', 'out1'])
        if debug:
            P.wait_all('sp', ['dbg_yb', 'dbg_ysc', 'dbg_ya', 'dbg_x1'])
        P.replay()
    return nc


def _prep_inputs(inputs, T, single=False):
    f = lambda k: np.asarray(inputs[k], np.float32)
    x = f('x')[:, :T]
    w_in = f('w_in')[0]
    mu = f('tshift_mu')[0]
    TQ = T if single else T // 4
    ws = f('gmlp_ws')[0]
    chunkcols = lambda v, n: np.ascontiguousarray(v.reshape(n, 128).T)

    def tok_units(w):
        K_ = w.shape[0]
        a = w.reshape(K_ // 2048, 16, 128, 8, 512).transpose(3, 0, 2, 1, 4)
        return np.ascontiguousarray(a).reshape(-1, 64, 8192)

    units = {}
    if f('w_out').shape[1] == D:
        units['wg'] = tok_units(w_in[:, :4096])
        units['wo'] = tok_units(f('w_out')[0])
        units['w2'] = tok_units(f('w_ff2')[0])
        units['w1'] = np.ascontiguousarray(f('w_ff1')[0].reshape(32, 128, -1, 256).transpose(2, 1, 0, 3)).reshape(-1, 64, 8192)
    maps = []
    for c in range(NCORE):
        b, g = c // 4, c % 4
        hs = slice(g * 512, (g + 1) * 512)
        base = 4096
        cols = np.concatenate([base + g * 512 + np.arange(512), base + 2048 + g * 512 + np.arange(512),
                               base + 4096 + g * 512 + np.arange(512), base + 6144 + np.arange(384)])
        w_rw = w_in[:, cols]
        wrw_h = np.ascontiguousarray(w_rw.reshape(KC, 128, 15, 128).transpose(2, 1, 0, 3)).reshape(15, 128, KC * 128)
        pp = np.zeros((128, NPP), np.float32)
        pp[:, PP_GPRE:PP_GPRE + 32] = chunkcols(f('pre_mix_g')[0], 32)
        pp[:, PP_GFFN:PP_GFFN + 32] = chunkcols(f('pre_ffn_g')[0], 32)
        pp[:, PP_MU:PP_MU + 15] = chunkcols(mu[cols - base], 15)
        pp[:, PP_W0:PP_W0 + 4] = chunkcols(f('decay_w0')[0][hs], 4)
        pp[:, PP_A0:PP_A0 + 4] = chunkcols(f('iclr_a0')[0][hs], 4)
        pp[:, PP_KK:PP_KK + 4] = chunkcols(f('k_k')[0][hs], 4)
        pp[:, PP_KA:PP_KA + 4] = chunkcols(f('k_a')[0][hs], 4)
        pp[:, PP_RK:PP_RK + 4] = chunkcols(f('r_k')[0].reshape(-1)[hs], 4)
        pp[:, PP_BS:PP_BS + 16] = f('gmlp_bs')[0].T
        fp = np.concatenate([f('lnx_g')[0][hs], f('lnx_b')[0][hs], f('gmlp_ln_g')[0], f('gmlp_ln_b')[0],
                             f('post_mix_g')[0], f('post_ffn_g')[0]])[None, :].astype(np.float32)
        lora = np.concatenate([f('decay_up')[0][:, hs], f('iclr_up')[0][:, hs]], axis=0)
        sel = np.zeros((128, 4), np.float32)
        sel[:, g] = 1.0
        m = {
            "xq": (x[b] if single else x[b, g * TQ:(g + 1) * TQ]), "wrw": wrw_h,
            "pp": pp, "fp": np.ascontiguousarray(fp), "lora_up": np.ascontiguousarray(lora),
            "gate_up": np.ascontiguousarray(f('gate_up')[0][:, hs]), "ws": ws, "sel": sel,
        }
        for nm, u in units.items():
            m[nm] = np.ascontiguousarray(u[c % 4::4]).reshape(-1, 8192)
        maps.append(m)
    return maps


_NC_CACHE = {}


def run(inputs, T, debug=False, stop=None, dff=DFF):
    key = (T, debug, stop, dff)
    if key not in _NC_CACHE:
        _NC_CACHE[key] = build_nc(T, debug, stop, dff=dff)
    nc = _NC_CACHE[key]
    maps = _prep_inputs(inputs, T)
    names = set()
    for a in nc.m.functions[0].allocations:
        if isinstance(a, mybir.MemoryLocationSet) and a.kind == "ExternalInput":
            names.add(a.memorylocations[0].name)
    maps = [{k: v for k, v in m.items() if k in names} for m in maps]
    res = run_bass_kernel_spmd(nc, maps, core_ids=list(range(NCORE)))
    TQ = T // 4
    out = np.zeros((2, T, D), np.float32)
    for c in range(NCORE):
        b, q = c // 4, c % 4
        out[b, q * TQ:(q + 1) * TQ] = res.results[c]["out"]
    return out, res


def kernel(**inputs):
    out, _ = run(inputs, 8192)
    return out
```

```python
import numpy as np
from contextlib import ExitStack
import concourse.bass as bass
import concourse.mybir as mybir
from concourse.bass_utils import run_bass_kernel_spmd

F32 = mybir.dt.float32
BF16 = mybir.dt.bfloat16
ALU = mybir.AluOpType
AF = mybir.ActivationFunctionType
AX = mybir.AxisListType

D = 4096
DFF = 16384
KC = 32
NCORE = 8
USE_F32R = False


class Prog:
    ENG = ('pe', 'act', 'dve', 'pool', 'sp')

    def __init__(self, nc, es):
        self.nc = nc
        self.es = es
        self.q = {e: [] for e in self.ENG}
        self.cnt = {e: 0 for e in self.ENG}
        self.sem = {e: es.enter_context(nc.semaphore("s_" + e)) for e in ('pe', 'act', 'dve', 'pool')}
        self.seen = {e: {} for e in self.ENG}
        self.state = {}
        self.dmas = {}
        self.psum_keys = set(['psT0', 'psT1', 'ps2', 'ps3', 'ps4', 'ps5', 'ps6', 'ps7', 'q0', 'q1', 'q2', 'q3', 'q4', 'q5', 'q6', 'q7'])

    def _deps(self, eng, reads, writes):
        deps = {}

        def add(tok):
            if tok is None:
                return
            s, v = tok
            if eng == 'pe' and s == 'pe':
                return
            if deps.get(s, 0) < v:
                deps[s] = v
        for k in reads:
            st = self.state.get(k)
            if st:
                add(st['w'])
                if k in self.psum_keys:
                    for rk, tok in st['r'].items():
                        if rk != eng:
                            add(tok)
        for k in writes:
            st = self.state.get(k)
            if st:
                add(st['w'])
                for rk, tok in st['r'].items():
                    add(tok)
        return deps

    def _emit_waits(self, eng, deps):
        for s, v in deps.items():
            if self.seen[eng].get(s, 0) >= v:
                continue
            self.seen[eng][s] = v
            semh = self.sem[s] if s in self.sem else self.dmas[s][0]
            self.q[eng].append(lambda e, semh=semh, v=v: e.wait_ge(semh, v))

    def _update(self, eng_key, tok, reads, writes):
        for k in reads:
            st = self.state.setdefault(k, {'w': None, 'r': {}})
            st['r'][eng_key] = tok
        for k in writes:
            self.state[k] = {'w': tok, 'r': {}}

    def op(self, eng, name, reads=(), writes=(), inc=True, **kw):
        deps = self._deps(eng, reads, writes)
        self._emit_waits(eng, deps)
        if inc:
            self.cnt[eng] += 1
            tok = (eng, self.cnt[eng])
            semh = self.sem[eng]
            self.q[eng].append(lambda e, name=name, kw=kw, semh=semh: getattr(e, name)(**kw).then_inc(semh, 1))
        else:
            assert eng == 'pe'
            tok = (eng, self.cnt[eng] + 1)
            self.q[eng].append(lambda e, name=name, kw=kw: getattr(e, name)(**kw))
        self._update(eng, tok, reads, writes)
        return tok

    def dma(self, eng, chan, reads=(), writes=(), **kw):
        return self.dma_start(eng, chan, lambda e, kw=kw: e.dma_start(**kw), reads, writes)

    def dma_start(self, eng, chan, fn, reads=(), writes=()):
        s = 'dma:' + chan
        if s not in self.dmas:
            self.dmas[s] = [self.es.enter_context(self.nc.semaphore("d_" + chan)), 0]
        deps = self._deps(eng, reads, writes)
        prev = self.dmas[s][1]
        if prev > 0 and deps.get(s, 0) < prev:
            deps[s] = prev
        self._emit_waits(eng, deps)
        self.dmas[s][1] += 16
        tok = (s, self.dmas[s][1])
        semh = self.dmas[s][0]
        self.q[eng].append(lambda e, fn=fn, semh=semh: fn(e).then_inc(semh, 16))
        self._update(s, tok, reads, writes)
        return tok

    def wait_all(self, eng, keys):
        self._emit_waits(eng, self._deps(eng, keys, keys))

    def alias(self, new_keys, old_keys):
        toks = {}
        for k in old_keys:
            st = self.state.get(k)
            if not st:
                continue
            for tok in [st['w']] + list(st['r'].values()):
                if tok is None:
                    continue
                if toks.get(tok[0], 0) < tok[1]:
                    toks[tok[0]] = tok[1]
        for k in new_keys:
            st = self.state.setdefault(k, {'w': None, 'r': {}})
            for i, (s, v) in enumerate(toks.items()):
                key = '_alias_' + s
                if st['r'].get(key, (s, 0))[1] < v:
                    st['r'][key] = (s, v)

    def barrier(self):
        deps = {e: self.cnt[e] for e in ('pe', 'act', 'dve', 'pool') if self.cnt[e] > 0}
        for s, (h, c) in self.dmas.items():
            if c > 0:
                deps[s] = c
        for e in self.ENG:
            d = {s: v for s, v in deps.items() if s != e}
            self._emit_waits(e, d)

    def replay(self):
        nc = self.nc
        q = self.q
        with nc.Block() as block:
            @block.tensor
            def _(e):
                for f in q['pe']:
                    f(e)

            @block.scalar
            def _(e):
                for f in q['act']:
                    f(e)

            @block.vector
            def _(e):
                for f in q['dve']:
                    f(e)

            @block.gpsimd
            def _(e):
                for f in q['pool']:
                    f(e)

            @block.sync
            def _(e):
                for f in q['sp']:
                    f(e)


class Arena:
    def __init__(self, ap, nwords):
        self.a = ap
        self.n = nwords
        self.off = 0

    def _shape(self, ap, shape):
        if len(shape) == 1:
            return ap
        if len(shape) == 2:
            return ap.rearrange("p (a b) -> p a b", a=shape[0], b=shape[1])
        return ap.rearrange("p (a b c) -> p a b c", a=shape[0], b=shape[1], c=shape[2])

    def f32(self, *shape):
        n = int(np.prod(shape))
        ap = self.a[:, self.off:self.off + n]
        self.off += n
        assert self.off <= self.n, (self.off, self.n)
        return self._shape(ap, shape)

    def bf16(self, *shape):
        n = int(np.prod(shape))
        w = (n + 1) // 2
        ap = self.a[:, self.off:self.off + w].bitcast(BF16)
        self.off += w
        assert self.off <= self.n, (self.off, self.n)
        return self._shape(ap, shape)


PP_GPRE = 0
PP_GFFN = 32
PP_MU = 64
PP_W0 = 79
PP_A0 = 83
PP_KK = 87
PP_KA = 91
PP_RK = 95
PP_BS = 99
NPP = 115
FP_LNXG = 0
FP_LNXB = 512
FP_LNG = 1024
FP_LNB = 3072
FP_GMIX = 5120
FP_GOUT = 9216
NFP = 13312

ARENA_WORDS = 51200


def build_nc(T, debug=False, stop=None, single=False, dff=DFF):
    TQ = T if single else T // 4
    NT1 = T // 512
    NT2 = TQ // 256
    nc = bass.Bass("TRN2", target_bir_lowering=False)
    es = ExitStack()
    with es:
        P = Prog(nc, es)

        def dram(name, shape, dt, kind="ExternalInput"):
            return nc.dram_tensor(name, shape, dt, kind=kind).ap()
        xq = dram("xq", [TQ, D], F32)
        wrw = dram("wrw", [15, 128, KC * 128], F32)
        big = stop not in ('p0', 'p1', 'p1a', 'p1b', 'p1c', 'p1d')
        NU = {'wg': 32, 'wo': 32, 'w1': dff // 128, 'w2': dff // 128}
        if big and not single:
            wsrc = {nm: dram(nm, [NU[nm] // 4 * 64, 8192], F32) for nm in NU}
        pp_d = dram("pp", [128, NPP], F32)
        fp_d = dram("fp", [1, NFP], F32)
        lora_d = dram("lora_up", [128, 512], F32)
        gate_d = dram("gate_up", [256, 512], F32)
        ws_d = dram("ws", [16, 128, 128], F32)
        out_d = dram("out", [TQ, D], F32, kind="ExternalOutput")
        wrw_bf = dram("wrw_bf", [15, 128, KC * 128], BF16, kind="Internal")
        if big:
            if single:
                wbf = {nm: dram(nm + "_bf", [NU[nm] * 64, 8192], BF16) for nm in NU}
            else:
                wsh = {nm: dram(nm + "_sh", [NU[nm] // 4 * 64, 8192], BF16, kind="Internal") for nm in NU}
                wbf = {nm: dram(nm + "_bf", [NU[nm] * 64, 8192], BF16, kind="Internal") for nm in NU}
            wslabs = {nm: wbf[nm].rearrange("(s p) e -> s p e", p=128) for nm in NU}
        PR = min(1024, T)
        h_loc = dram("h_loc", [TQ, D], BF16, kind="Internal")
        h_full = h_loc if single else dram("h_full", [T, D], BF16, kind="Internal")
        yb_loc = dram("yb_loc", [T, 512], BF16, kind="Internal")
        yb_gat = dram("yb_gat", [4 * T, 512], BF16, kind="Internal")
        dbg = {}
        if debug:
            dbg['yb'] = dram("dbg_yb", [T, 512], F32, kind="ExternalOutput")
            dbg['ysc'] = dram("dbg_ysc", [T, 512], F32, kind="ExternalOutput")
            dbg['x1'] = dram("dbg_x1", [TQ, D], F32, kind="ExternalOutput")
            dbg['ya'] = dram("dbg_ya", [TQ, D], F32, kind="ExternalOutput")

        arena_t = es.enter_context(nc.sbuf_tensor("arena", [128, ARENA_WORDS], F32))
        PS = [es.enter_context(nc.psum_tensor("psb%d" % i, [128, 512], F32)) for i in range(8)]
        ccsem = es.enter_context(nc.semaphore("ccsem"))

        def V(name, r, w, **kw):
            P.op('dve', name, r, w, **kw)

        def G(name, r, w, **kw):
            P.op('pool', name, r, w, **kw)

        def ACT(out, in_, func, r, w, **kw):
            P.op('act', 'activation', r, w, out=out, in_=in_, func=func, **kw)

        def MM(out, lhsT, rhs, start, stop, r, w, inc=True):
            P.op('pe', 'matmul', r, w, inc=inc, out=out, lhsT=lhsT, rhs=rhs, start=start, stop=stop)

        def TR(out, in_, identity, r, w, inc=True):
            P.op('pe', 'transpose', r, w, inc=inc, out=out, in_=in_, identity=identity)

        F32R = mybir.dt.float32r

        def R32(ap):
            return ap.bitcast(F32R) if USE_F32R else ap

        def bc3(ap2, n):
            return ap2.unsqueeze(2).broadcast_to([128, ap2.shape[1], n])

        def cast(chan, dst, src, cols, key):
            if cols > 2048:
                s2 = src.rearrange("r (a c) -> (r a) c", c=2048)
                d2 = dst.rearrange("r (a c) -> (r a) c", c=2048)
            else:
                s2, d2 = src, dst
            nr = s2.shape[0]
            step = 8192
            keys = []
            for i, r0 in enumerate(range(0, nr, step)):
                r1 = min(nr, r0 + step)
                P.dma('pool', chan + str(i % 2), [], [key + ".%d" % i], out=d2[r0:r1, :], in_=s2[r0:r1, :])
                keys.append(key + ".%d" % i)
            return keys

        k_wrw = cast('c0', wrw_bf.rearrange("a p n -> (a p) n"), wrw.rearrange("a p n -> (a p) n"), 4096, 'wrw_bf')

        A = Arena(arena_t, ARENA_WORDS)
        identb = A.bf16(128)
        identf = A.f32(128)
        mask4 = A.f32(512)
        maskL = A.f32(128)
        bones = A.f32(128)
        Esel = A.f32(2)
        scanmask = A.f32(512)
        pp = A.f32(NPP)
        epsc = A.f32(4)
        lnxg = A.f32(512)
        lnxb = A.f32(512)
        lora_up = A.bf16(512)
        gate_up = A.bf16(2, 512)
        H2 = A.f32(4, 64)
        plast = A.f32(16)
        small = A.f32(64)
        xt_off = A.off
        xt = A.f32(4096)
        xn = A.bf16(4096)
        hT = A.bf16(KC, 512)
        wsl = [A.bf16(KC, 128) for _ in range(2)]
        praw = [A.f32(513) for _ in range(2)]
        tmpd = A.f32(512)
        pl = A.f32(512)
        pg = A.f32(2, 512)
        lt = A.bf16(512)
        sgT = A.bf16(2, 512)
        prkv = A.f32(3, 512)
        gtok = A.f32(4, 512)
        ytok = A.f32(4, 512)
        vtok = A.f32(4, 512)
        bon = A.f32(4, 8)
        sw = A.f32(512)
        cum = A.f32(512)
        cumx = A.f32(512)
        Gin = A.f32(512)
        Gex = A.f32(512)
        Ginv = A.f32(512)
        Grat = A.f32(512)
        GC = A.f32(4)
        aa = A.f32(512)
        kkn = A.f32(512)
        km = A.f32(512)
        t2 = A.f32(512)
        ar = A.f32(2, 512)
        bbar = A.f32(512)
        kbar = A.f32(512)
        btil = A.f32(512)
        ktil = A.f32(512)
        rk = A.f32(512)
        tokmS = [A.f32(4, 128) for _ in range(2)]
        zsbS = [A.f32(64) for _ in range(4)]
        tauS = [A.f32(128) for _ in range(4)]
        PTsS = [A.f32(64) for _ in range(4)]
        GTsS = [A.f32(128) for _ in range(4)]
        scS = [A.f32(512) for _ in range(2)]
        W3S = [[A.f32(384) for _ in range(2)] for _ in range(1)]
        fin = [A.f32(512) for _ in range(3 if debug else 2)]
        Ax = Arena(arena_t, ARENA_WORDS)
        Ax.off = xt_off
        scS += [Ax.f32(512) for _ in range(2)]
        W3S += [[Ax.f32(384) for _ in range(2)] for _ in range(3)]
        assert Ax.off <= xt_off + 4096
        UB = [(6, 7), (2, 3), (5, 1), (0, 4)]
        BK = ['psT0', 'psT1', 'ps2', 'ps3', 'ps4', 'ps5', 'ps6', 'ps7']
        ygb = A.bf16(512)
        st8 = A.f32(32)

        G('memset', [], ['const'], ap=identf, constant=1.0)
        G('affine_select', ['const'], ['const'], out=identf, in_=identf, pattern=[[-1, 128]], compare_op=ALU.is_equal,
          fill=0.0, base=0, channel_multiplier=1)
        G('tensor_copy', ['const'], ['const'], out=identb, in_=identf)
        G('memset', ['const'], ['const'], ap=mask4, constant=1.0)
        for i in range(4):
            G('affine_select', ['const'], ['const'], out=mask4[:, i * 128:(i + 1) * 128], in_=mask4[:, i * 128:(i + 1) * 128],
              pattern=[[1, 128]], compare_op=ALU.is_ge, fill=0.0, base=(-1 if i % 2 == 0 else 0), channel_multiplier=-1)
        G('memset', ['const'], ['const'], ap=maskL, constant=1.0)
        G('affine_select', ['const'], ['const'], out=maskL, in_=maskL, pattern=[[-1, 128]], compare_op=ALU.is_ge,
          fill=0.0, base=-1, channel_multiplier=1)
        G('memset', ['const'], ['const'], ap=bones, constant=0.0)
        G('memset', ['const'], ['const'], ap=bones[0:64, 0:64], constant=1.0)
        G('memset', ['const'], ['const'], ap=bones[64:128, 64:128], constant=1.0)
        G('memset', ['const'], ['const'], ap=Esel, constant=0.0)
        G('memset', ['const'], ['const'], ap=Esel[0:64, 0:1], constant=1.0)
        G('memset', ['const'], ['const'], ap=Esel[64:128, 1:2], constant=1.0)
        G('memset', ['const'], ['const'], ap=scanmask, constant=1.0)
        for c in range(4):
            G('memset', ['const'], ['const'], ap=scanmask[:, c * 128:c * 128 + 1], constant=0.0)
        G('memset', ['const'], ['const'], ap=epsc[:, 0:1], constant=1e-6)
        G('memset', ['const'], ['const'], ap=epsc[:, 1:2], constant=64e-5)
        G('memset', [], ['H%d' % h for h in range(8)], ap=H2, constant=0.0)
        G('memset', [], ['plast'], ap=plast, constant=0.0)
        P.dma('sp', 'pp', [], ['pp'], out=pp, in_=pp_d)
        P.dma('sp', 'lnxg', [], ['lnxg'], out=lnxg, in_=fp_d[:, FP_LNXG:FP_LNXG + 512].partition_broadcast(128))
        P.dma('sp', 'lnxb', [], ['lnxb'], out=lnxb, in_=fp_d[:, FP_LNXB:FP_LNXB + 512].partition_broadcast(128))
        P.dma('pool', 'lora', [], ['lora_up'], out=lora_up, in_=lora_d)
        P.dma('pool', 'gateup', [], ['gate_up'], out=gate_up, in_=gate_d.rearrange("(j p) n -> p j n", p=128))

        ncc = [0]

        def collective(groups, src, dst, rkeys, wkeys):
            P.wait_all('pool', rkeys)
            ncc[0] += 1
            P.q['pool'].append(lambda e, src=src, dst=dst, groups=groups: e.collective_compute(
                "AllGather", ALU.bypass, replica_groups=groups, ins=[src], outs=[dst]).then_inc(ccsem, 1))
            n = ncc[0]
            return n

        def wait_cc(n, engines=None):
            for en in (engines or P.ENG):
                P.q[en].append(lambda e, n=n: e.wait_ge(ccsem, n))

        if stop == 'p0':
            P.barrier()
            P.replay()
            return nc
        psT_b = [PS[0].bitcast(BF16), PS[1].bitcast(BF16)]

        def rms_rows(src, dst, ksrc, kdst, sm, epsap, kc_):
            ss, rs = sm[:, 0:1], sm[:, 1:2]
            ACT(dst, src, AF.Square, [ksrc], [kdst, 'ss'], accum_out=ss)
            ACT(rs, ss, AF.Sqrt, ['ss', kc_], ['rs'], scale=1.0 / D, bias=epsap)
            V('reciprocal', ['rs'], ['rs'], out=rs, in_=rs)
            V('tensor_scalar', [ksrc, 'rs'], [kdst], out=dst, in0=src, scalar1=rs, scalar2=None, op0=ALU.mult)

        def transposes(src, dstT, col0, gains, ksrc, kdst, idb, kconst, kps):
            for b4 in range(4):
                bank = b4 % 2
                for i in range(8):
                    kc = b4 * 8 + i
                    TR(psT_b[bank][:, i * 128:(i + 1) * 128], src[:, kc * 128:(kc + 1) * 128], idb,
                       [ksrc, kconst], [kps % bank], inc=(i == 7))
                dst = dstT[:, b4 * 8:(b4 + 1) * 8, col0:col0 + 128]
                src3 = psT_b[bank][:, :].rearrange("p (a b) -> p a b", a=8, b=128)
                if gains is None:
                    ACT(dst, src3, AF.Copy, [kps % bank], [kdst])
                else:
                    V('tensor_tensor', [kps % bank, gains[1]], [kdst], out=dst, in0=src3,
                      in1=bc3(gains[0][:, b4 * 8:(b4 + 1) * 8], 128), op=ALU.mult)

        slab_i = [0]

        def project(ch, dst, kdst):
            si = slab_i[0] % 2
            slab_i[0] += 1
            ks = 'wsl%d' % si
            P.dma('sp', ks, k_wrw, [ks], out=wsl[si], in_=wrw_bf[ch].rearrange("p (k n) -> p k n", k=KC))
            bank = 2 + si
            kp = 'ps%d' % bank
            for kc in range(KC):
                MM(PS[bank][:, :], wsl[si][:, kc, :], hT[:, kc, :], kc == 0, kc == KC - 1, [ks, 'hT'], [kp], inc=(kc == KC - 1))
            pr = praw[si]
            kr = 'praw%d' % si
            ACT(pr[:, 1:513], PS[bank][:, :], AF.Copy, [kp], [kr])
            G('tensor_copy', ['plast', kr], [kr], out=pr[:, 0:1], in_=plast[:, ch:ch + 1])
            V('tensor_tensor', [kr], ['tmpd'], out=tmpd, in0=pr[:, 0:512], in1=pr[:, 1:513], op=ALU.subtract)
            V('scalar_tensor_tensor', ['tmpd', 'pp', kr], [kdst], out=dst, in0=tmpd, scalar=pp[:, PP_MU + ch:PP_MU + ch + 1],
              in1=pr[:, 1:513], op0=ALU.mult, op1=ALU.add)
            G('tensor_copy', [kr, 'plast'], ['plast'], out=plast[:, ch:ch + 1], in_=pr[:, 512:513])

        for st_ in range(TQ // 128):
            P.dma('sp', 'xt', [], ['xt'], out=xt, in_=xq[st_ * 128:(st_ + 1) * 128, :])
            rms_rows(xt, xn, 'xt', 'xn', small, epsc[:, 0:1], 'const')
            P.dma('sp', 'hloc', ['xn'], ['h_loc%d' % st_], out=h_loc[st_ * 128:(st_ + 1) * 128, :], in_=xn)
        cc_h0 = ncc[0]
        if not single:
            for i in range(TQ // 128):
                cc_h = collective([[0, 1, 2, 3], [4, 5, 6, 7]], h_loc[i * 128:(i + 1) * 128, :], h_full[i * 512:(i + 1) * 512, :],
                                  ['h_loc%d' % i], ['h_full'])
        cc_w = {}
        if big and not single and stop != 'pA':
            print("phase-1 arena words used", A.off)
            if ARENA_WORDS - A.off >= 8192:
                stg = [A.bf16(8192) for i in range(2)]
                own_stg = True
            else:
                stg = [hT.rearrange("p a b -> p (a b)")[:, i * 8192:(i + 1) * 8192] for i in range(2)]
                own_stg = False
            nstg = 0
            for nm in ('wg', 'wo', 'w1', 'w2'):
                nur = NU[nm] // 4
                for jj in range(nur // 2):
                    sg_ = stg[nstg % 2]
                    kk_ = 'stg%d' % (nstg % 2)
                    nstg += 1
                    P.dma('pool', kk_, [], [kk_], out=sg_.rearrange("p (a c) -> p a c", a=4),
                          in_=wsrc[nm][jj * 128:(jj + 1) * 128, :].rearrange("p (a c) -> p a c", a=4))
                    P.dma('sp', kk_ + 'o', [kk_], [nm + '_sh%d' % jj], out=wsh[nm][jj * 128:(jj + 1) * 128, :], in_=sg_)
                for j in range(nur):
                    cc_w[nm] = collective([[0, 1, 2, 3], [4, 5, 6, 7]], wsh[nm][j * 64:(j + 1) * 64, :], wbf[nm][j * 256:(j + 1) * 256, :],
                                          [nm + '_sh%d' % (j // 2)], [nm + '_bf'])
            if not own_stg:
                P.alias(['hT'], ['stg0', 'stg1'])
        if not single and stop in ('pA', 'pB'):
            wait_cc(ncc[0])
        if stop in ('pA', 'pB'):
            P.barrier()
            wait_cc(ncc[0])
            P.replay()
            return nc

        class _Stop(Exception):
            pass
        ydone = set()

        def pending_block():
            return stop in ('p1', 'p1a', 'p1b', 'p1c', 'p1d', 'p1big')
        P.alias(['sc2', 'sc3'] + ['W3_%d_%d' % (u, k) for u in (1, 2, 3) for k in (0, 1)], ['xt'])

        def phase1():
          for tt in range(NT1):
              t0 = tt * 512
              if stop == 'p1a' and tt == 1:
                  raise _Stop()
              for s in range(4):
                  st_ = (t0 // 128) + s
                  hrow = st_ * 128 if single else (st_ % (TQ // 128)) * 512 + (st_ // (TQ // 128)) * 128
                  if not single and tt * 4 + s < TQ // 128:
                      wait_cc(cc_h0 + (st_ % (TQ // 128)) + 1, ['sp'])
                  P.dma('sp', 'xn', [], ['xn'], out=xn, in_=h_full[hrow:hrow + 128, :])
                  transposes(xn, hT, s * 128, (pp[:, PP_GPRE:PP_GPRE + 32], 'pp'), 'xn', 'hT', identb, 'const', 'psT%d')
              if stop == 'p1a':
                  continue

              project(12, pl, 'pl')
              project(13, pg[:, 0, :], 'pg0')
              project(14, pg[:, 1, :], 'pg1')
              ACT(lt[0:64, :], pl[0:64, :], AF.Tanh, ['pl'], ['lt'])
              ACT(lt[64:128, :], pl[64:128, :], AF.Copy, ['pl', 'lt'], ['lt'])
              ACT(sgT, pg, AF.Sigmoid, ['pg0', 'pg1'], ['sgT'])
              for c in range(4):
                  bank = 2 + (c % 2)
                  for jc in range(2):
                      MM(PS[bank][:, :], sgT[:, jc, c * 128:(c + 1) * 128], gate_up[:, jc, :], jc == 0, jc == 1,
                         ['sgT', 'gate_up'], ['ps%d' % bank], inc=(jc == 1))
                  ACT(gtok[:, c, :], PS[bank][:, :], AF.Copy, ['ps%d' % bank], ['gtok%d' % c])
              if stop == 'p1b' and c == 3:
                  raise _Stop()

              for j in range(4):
                  project(j, prkv[:, 0, :], 'pr')
                  project(4 + j, prkv[:, 1, :], 'pk')
                  project(8 + j, prkv[:, 2, :], 'pv')
                  rs_, ks_, vs_ = prkv[:, 0, :], prkv[:, 1, :], prkv[:, 2, :]

                  def col(base, j=j):
                      return pp[:, base + j:base + j + 1]
                  MM(PS[4][:, :], lora_up[0:64, j * 128:(j + 1) * 128], lt[0:64, :], True, True, ['lora_up', 'lt'], ['ps4'])
                  ACT(sw, PS[4][:, :], AF.Sigmoid, ['ps4', 'pp'], ['sw'], bias=col(PP_W0))
                  G('tensor_scalar', ['sw'], ['sw'], out=sw, in0=sw, scalar1=-0.6065306597126334, scalar2=None, op0=ALU.mult)
                  V('tensor_tensor_scan', ['sw', 'const'], ['cum'], out=cum, data0=scanmask, data1=sw, initial=0.0,
                    op0=ALU.mult, op1=ALU.add)
                  G('tensor_tensor', ['cum', 'sw'], ['cumx'], out=cumx, in0=cum, in1=sw, op=ALU.subtract)
                  ACT(Gin, cum, AF.Exp, ['cum'], ['Gin'])
                  ACT(Gex, cumx, AF.Exp, ['cumx'], ['Gex'])
                  ACT(Ginv, cum, AF.Exp, ['cum'], ['Ginv'], scale=-1.0)
                  cumC = cum[:, 127::128]
                  V('tensor_tensor', ['cum'], ['Grat'], out=Grat.rearrange("p (c t) -> p c t", c=4), in0=bc3(cumC, 128),
                    in1=cum.rearrange("p (c t) -> p c t", c=4), op=ALU.subtract)
                  ACT(Grat, Grat, AF.Exp, ['Grat'], ['Grat'])
                  ACT(GC, cumC, AF.Exp, ['cum'], ['GC'])
                  MM(PS[4][:, :], lora_up[64:128, j * 128:(j + 1) * 128], lt[64:128, :], True, True, ['lora_up', 'lt'], ['ps4'])
                  ACT(aa, PS[4][:, :], AF.Sigmoid, ['ps4', 'pp'], ['aa'], bias=col(PP_A0))
                  G('tensor_scalar', ['pk', 'pp'], ['kkn'], out=kkn, in0=ks_, scalar1=col(PP_KK), scalar2=None, op0=ALU.mult)
                  G('tensor_tensor', ['kkn'], ['t2'], out=t2, in0=kkn, in1=kkn, op=ALU.mult)
                  MM(PS[4][:, :], bones, t2, True, True, ['const', 't2'], ['ps4'])
                  ACT(t2, PS[4][:, :], AF.Sqrt, ['ps4'], ['t2'])
                  V('tensor_scalar', ['t2'], ['t2'], out=t2, in0=t2, scalar1=1e-12, scalar2=None, op0=ALU.max)
                  V('reciprocal', ['t2'], ['t2'], out=t2, in_=t2)
                  V('tensor_tensor', ['kkn', 't2'], ['kkn'], out=kkn, in0=kkn, in1=t2, op=ALU.mult)
                  V('tensor_scalar', ['aa', 'pp'], ['km'], out=km, in0=aa, scalar1=col(PP_KA), scalar2=col(PP_KA),
                    op0=ALU.mult, op1=ALU.subtract)
                  V('scalar_tensor_tensor', ['km', 'pk'], ['km'], out=km, in0=km, scalar=1.0, in1=ks_, op0=ALU.add, op1=ALU.mult)
                  V('scalar_tensor_tensor', ['kkn', 'Gex'], ['ar0'], out=ar[:, 0, :], in0=kkn, scalar=-1.0, in1=Gex,
                    op0=ALU.mult, op1=ALU.mult)
                  G('tensor_tensor', ['pr', 'Gin'], ['ar1'], out=ar[:, 1, :], in0=rs_, in1=Gin, op=ALU.mult)
                  G('tensor_tensor', ['kkn', 'aa', 't2'], ['t2'], out=t2, in0=kkn, in1=aa, op=ALU.mult)
                  V('tensor_tensor', ['t2', 'Ginv'], ['bbar'], out=bbar, in0=t2, in1=Ginv, op=ALU.mult)
                  G('tensor_tensor', ['km', 'Ginv'], ['kbar'], out=kbar, in0=km, in1=Ginv, op=ALU.mult)
                  V('tensor_tensor', ['t2', 'Grat'], ['btil'], out=btil, in0=t2, in1=Grat, op=ALU.mult)
                  G('tensor_tensor', ['km', 'Grat'], ['ktil'], out=ktil, in0=km, in1=Grat, op=ALU.mult)
                  V('scalar_tensor_tensor', ['pr', 'pp', 'km'], ['rk'], out=rk, in0=rs_, scalar=col(PP_RK), in1=km,
                    op0=ALU.mult, op1=ALU.mult)

                  KCH = 2
                  for cg0 in range(0, 4, KCH):
                      for cc in range(KCH):
                          c = cg0 + cc
                          cs = slice(c * 128, (c + 1) * 128)
                          tk_ = 'tokm%d' % cc
                          MM(PS[4][:, 0:2], rk[:, cs], Esel, True, True, ['rk', 'const'], ['ps4'])
                          ACT(bon[:, c, 2 * j:2 * j + 2], PS[4][:, 0:2], AF.Copy, ['ps4'], ['bon%d' % c])
                          srcs = [(ar[:, 0, cs], 'ar0'), (vs_[:, cs], 'pv'), (btil[:, cs], 'btil'), (ktil[:, cs], 'ktil')]
                          for i, (sap, skey) in enumerate(srcs):
                              TR(PS[4][:, i * 128:(i + 1) * 128], sap, identf, [skey, 'const'], ['ps4'], inc=(i == 3))
                          ACT(tokmS[cc], PS[4][:, :].rearrange("p (a b) -> p a b", a=4, b=128), AF.Copy, ['ps4'], [tk_])
                          G('tensor_copy', [tk_], ['vtok%d' % c], out=vtok[:, c, j * 128:(j + 1) * 128], in_=tokmS[cc][:, 1, :])

                      def partA(u, cc, i, j=j, cg0=cg0):
                          c = cg0 + cc
                          cs = slice(c * 128, (c + 1) * 128)
                          pb = slice(64 * i, 64 * i + 64)
                          tk_ = 'tokm%d' % cc
                          Atok, Vtok, Btok, Ktok = (tokmS[cc][:, n, pb] for n in range(4))
                          PA, PB = PS[UB[u][0]], PS[UB[u][1]]
                          ka, kb = BK[UB[u][0]], BK[UB[u][1]]
                          sc_, W3_ = scS[u], W3S[u]
                          ksc = 'sc%d' % u
                          MM(PA[:, 0:256], bbar[pb, cs], ar[pb, :, cs], True, True, ['bbar', 'ar0', 'ar1'], [ka], inc=False)
                          MM(PA[:, 256:512], kbar[pb, cs], ar[pb, :, cs], True, True, ['kbar', 'ar0', 'ar1'], [ka])
                          MM(PB[:, 384:512], ar[pb, 0, cs], bbar[pb, cs], True, True, ['bbar', 'ar0'], [kb])
                          V('tensor_tensor', [ka, 'const'], [ksc], out=sc_, in0=PA[:, :], in1=mask4, op=ALU.mult)
                          k0, k1 = 'W3_%d_0' % u, 'W3_%d_1' % u
                          V('tensor_tensor', [kb, 'const'], [k0], out=W3_[0][:, 0:128], in0=PB[:, 384:512], in1=maskL, op=ALU.mult)
                          G('tensor_copy', [ksc, k0], [k0], out=W3_[0][:, 128:256], in_=sc_[:, 0:128])
                          G('tensor_copy', ['const', k0], [k0], out=W3_[0][:, 256:384], in_=identf)
                          yield
                          for rnd in range(7):
                              wi, wo_ = W3_[rnd % 2], W3_[(rnd + 1) % 2]
                              ki, ko = (k0, k1) if rnd % 2 == 0 else (k1, k0)
                              if rnd < 6:
                                  MM(PB[:, 128:384], R32(wi[:, 0:128]), R32(wi[:, 128:384]), True, True, [ki], [kb], inc=False)
                                  MM(PB[:, 0:128], R32(wi[:, 128:256]), R32(wi[:, 0:128]), True, True, [ki], [kb])
                                  ACT(wo_[:, 0:256], PB[:, 0:256], AF.Copy, [kb], [ko])
                              else:
                                  MM(PB[:, 256:384], R32(wi[:, 0:128]), R32(wi[:, 256:384]), True, True, [ki], [kb])
                              V('tensor_tensor', [kb, ki, ko], [ko], out=wo_[:, 256:384], in0=PB[:, 256:384],
                                in1=wi[:, 256:384], op=ALU.add)
                              yield
                          X = W3_[1][:, 256:384]
                          kX = k1
                          Mak, Mrb = sc_[:, 256:384], sc_[:, 128:256]
                          MM(PA[:, 0:64], Mak, Vtok, True, True, [ksc, tk_], [ka])
                          ACT(zsbS[u], PA[:, 0:64], AF.Copy, [ka], ['zsb%d' % u])
                          yield
                          MM(PA[:, 64:128], X, Atok, True, True, [kX, tk_], [ka], inc=False)
                          MM(PA[:, 128:192], X, zsbS[u], True, True, [kX, 'zsb%d' % u], [ka])
                          V('tensor_copy', [ka], ['tau%d' % u], out=tauS[u], in_=PA[:, 64:192])
                          yield
                          MM(PA[pb, 192:256], tauS[u][:, 0:64], Btok, True, True, ['tau%d' % u, tk_], [ka], inc=False)
                          MM(PA[pb, 256:384], tauS[u][:, 0:64], Mrb, True, True, ['tau%d' % u, ksc], [ka])
                          ACT(PTsS[u][pb, :], PA[pb, 192:256], AF.Copy, [ka], ['PTs%d' % u])
                          V('tensor_tensor', [ka, 'ar1'], ['GTs%d' % u], out=GTsS[u][pb, :], in0=PA[pb, 256:384], in1=ar[pb, 1, cs], op=ALU.add)
                          yield

                      def partB(u, cc, i, j=j, cg0=cg0):
                          c = cg0 + cc
                          hh = 2 * j + i
                          Hk = 'H%d' % hh
                          pb = slice(64 * i, 64 * i + 64)
                          tk_ = 'tokm%d' % cc
                          Atok, Vtok, Btok, Ktok = (tokmS[cc][:, n, pb] for n in range(4))
                          PA, PB = PS[UB[u][0]], PS[UB[u][1]]
                          ka, kb = BK[UB[u][0]], BK[UB[u][1]]
                          sc_ = scS[u]
                          ksc = 'sc%d' % u
                          Mrb, Mrk = sc_[:, 128:256], sc_[:, 384:512]
                          MM(PA[:, 384:448], Mrb, tauS[u][:, 64:128], True, False, [ksc, 'tau%d' % u], [ka], inc=False)
                          MM(PA[:, 384:448], Mrk, Vtok, False, False, [ksc, tk_], [ka], inc=False)
                          MM(PA[:, 384:448], GTsS[u][pb, :], H2[pb, j, :], False, True, ['GTs%d' % u, Hk], [ka])
                          ACT(ytok[:, c, hh * 64:(hh + 1) * 64], PA[:, 384:448], AF.Copy, [ka], ['ytok%d' % c])
                          MM(PB[pb, 0:64], Btok, tauS[u][:, 64:128], True, False, [tk_, 'tau%d' % u], [kb], inc=False)
                          MM(PB[pb, 0:64], Ktok, Vtok, False, False, [tk_], [kb], inc=False)
                          MM(PB[pb, 0:64], PTsS[u][pb, :], H2[pb, j, :], False, True, ['PTs%d' % u, Hk], [kb])
                          V('scalar_tensor_tensor', [Hk, 'GC', kb], [Hk], out=H2[pb, j, :], in0=H2[pb, j, :],
                            scalar=GC[pb, c:c + 1], in1=PB[pb, 0:64], op0=ALU.mult, op1=ALU.add)

                      gens = [partA(cc * 2 + i, cc, i) for cc in range(KCH) for i in range(2)]
                      while gens:
                          for g_ in list(gens):
                              try:
                                  next(g_)
                              except StopIteration:
                                  gens.remove(g_)
                      for cc in range(KCH):
                          for i in range(2):
                              partB(cc * 2 + i, cc, i)
                      if stop == 'p1d':
                          raise _Stop()

              for c in range(4):
                  y3 = ytok[:, c, :].rearrange("p (h n) -> p h n", h=8)
                  f0, f1 = fin[0], fin[1]
                  f2 = fin[2] if debug else None
                  f03 = f0.rearrange("p (h n) -> p h n", h=8)
                  f13 = f1.rearrange("p (h n) -> p h n", h=8)
                  mu8, var8, rs8 = st8[:, 0:8], st8[:, 8:16], st8[:, 16:24]
                  yk = 'ytok%d' % c
                  r0 = t0 + c * 128
                  if debug:
                      P.dma('sp', 'dbg1', [yk], ['dbg_ysc'], out=dbg['ysc'][r0:r0 + 128, :], in_=ytok[:, c, :])
                  V('tensor_reduce', [yk], ['mu8'], out=mu8, in_=y3, axis=AX.X, op=ALU.add)
                  V('tensor_scalar', ['mu8'], ['mu8'], out=mu8, in0=mu8, scalar1=1.0 / 64, scalar2=None, op0=ALU.mult)
                  V('tensor_tensor', [yk, 'mu8'], ['f0'], out=f03, in0=y3, in1=bc3(mu8, 64), op=ALU.subtract)
                  G('tensor_tensor', ['f0'], ['f1'], out=f1, in0=f0, in1=f0, op=ALU.mult)
                  V('tensor_reduce', ['f1'], ['var8'], out=var8, in_=f13, axis=AX.X, op=ALU.add)
                  ACT(rs8, var8, AF.Sqrt, ['var8', 'const'], ['rs8'], scale=1.0 / 64, bias=epsc[:, 1:2])
                  V('reciprocal', ['rs8'], ['rs8'], out=rs8, in_=rs8)
                  V('tensor_tensor', ['f0', 'rs8'], ['f0'], out=f03, in0=f03, in1=bc3(rs8, 64), op=ALU.mult)
                  G('tensor_tensor', ['f0', 'lnxg'], ['f0'], out=f0, in0=f0, in1=lnxg, op=ALU.mult)
                  G('tensor_tensor', ['f0', 'lnxb'], ['f0'], out=f0, in0=f0, in1=lnxb, op=ALU.add)
                  v3 = vtok[:, c, :].rearrange("p (h n) -> p h n", h=8)
                  V('tensor_tensor', ['vtok%d' % c, 'bon%d' % c, 'f1'], ['f1'], out=f13, in0=v3, in1=bc3(bon[:, c, :], 64), op=ALU.mult)
                  G('tensor_tensor', ['f0', 'f1'], ['f0'], out=f0, in0=f0, in1=f1, op=ALU.add)
                  V('tensor_tensor', ['f0', 'gtok%d' % c], ['ygb'], out=ygb, in0=f0, in1=gtok[:, c, :], op=ALU.mult)
                  P.dma('sp', 'ygb', ['ygb'], ['yb_loc%d' % (r0 // PR)], out=yb_loc[r0:r0 + 128, :], in_=ygb)
                  if debug:
                      V('tensor_tensor', ['f0', 'gtok%d' % c], ['f2'], out=f2, in0=f0, in1=gtok[:, c, :], op=ALU.mult)
                      P.dma('sp', 'dbg2', ['f2'], ['dbg_yb'], out=dbg['yb'][r0:r0 + 128, :], in_=f2)
              if not single and (t0 + 512) % PR == 0 and not pending_block():
                  ip = (t0 + 512) // PR - 1
                  collective([[0, 1, 2, 3], [4, 5, 6, 7]], yb_loc[ip * PR:(ip + 1) * PR, :], yb_gat[ip * 4 * PR:(ip + 1) * 4 * PR, :],
                             ['yb_loc%d' % ip], ['yb_gat'])
                  ydone.add(ip)

        try:
            phase1()
        except _Stop:
            pass

        P.barrier()
        if stop in ('p1', 'p1a', 'p1b', 'p1c', 'p1d', 'p1big'):
            P.replay()
            return nc
        for i in range(0 if single else T // PR):
            if i in ydone:
                continue
            cc_y = collective([[0, 1, 2, 3], [4, 5, 6, 7]], yb_loc[i * PR:(i + 1) * PR, :], yb_gat[i * 4 * PR:(i + 1) * 4 * PR, :],
                              ['yb_loc%d' % i], ['yb_gat'])
        cc_y = ncc[0]
        if not single:
            wait_cc(cc_y)

        if stop == 'cc':
            P.replay()
            return nc
        A2 = Arena(arena_t, ARENA_WORDS)
        identb2 = A2.bf16(128)
        identf2 = A2.f32(128)
        pp2 = A2.f32(NPP)
        epsc2 = A2.f32(4)
        small2 = A2.f32(64)
        wsT = A2.bf16(16, 128)
        xres = A2.f32(2, 4096)
        ra0 = A2.off
        hT2 = A2.bf16(KC, 256)
        tmb = [A2.bf16(4096) for _ in range(2)]
        assert A2.off - ra0 == 8192
        rb0 = A2.off
        f1T = A2.bf16(128, 256)
        rb_end = A2.off
        slab = [A2.bf16(16, 512) for _ in range(3)]
        gbc = A2.f32(4096)
        rtmp = [A2.f32(256) for _ in range(2)]
        Aa = Arena(arena_t, ARENA_WORDS)
        Aa.off = ra0
        TMg = [Aa.f32(4096) for _ in range(2)]
        Ab = Arena(arena_t, ARENA_WORDS)
        Ab.off = rb0
        TMc = [Ab.f32(4096) for _ in range(2)]
        lng = Ab.f32(2048)
        lnb = Ab.f32(2048)
        vln = Ab.bf16(2048)
        cand = [Ab.bf16(4, 512) for _ in range(2)]
        wstmp = Ab.f32(128)
        assert Ab.off <= rb_end, (Ab.off, rb_end)

        G('memset', [], ['c2'], ap=identf2, constant=1.0)
        G('affine_select', ['c2'], ['c2'], out=identf2, in_=identf2, pattern=[[-1, 128]], compare_op=ALU.is_equal,
          fill=0.0, base=0, channel_multiplier=1)
        G('tensor_copy', ['c2'], ['c2'], out=identb2, in_=identf2)
        G('memset', ['c2'], ['c2'], ap=epsc2[:, 0:1], constant=1e-6)
        G('memset', ['c2'], ['c2'], ap=epsc2[:, 2:3], constant=1e-5)
        P.dma('sp', 'pp', [], ['pp2'], out=pp2, in_=pp_d)
        for h in range(16):
            P.dma('sp', 'wstmp', [], ['wstmp'], out=wstmp, in_=ws_d[h])
            TR(PS[0][:, 0:128], wstmp, identf2, ['wstmp', 'c2'], ['q0'])
            ACT(wsT[:, h, :], PS[0][:, 0:128], AF.Copy, ['q0'], ['wsT'])
            G('memset', ['wsT'], ['wsT'], ap=wsT[64:128, h, 0:64], constant=0.0)

        slab_n = [0]

        def tokmajor_matmul(wname, wkeys, lhsT_of, nk16, lhs_keys, evac):
            for cg in range(8):
                banks = (2 + 2 * (cg % 2), 3 + 2 * (cg % 2))
                for ks in range(nk16):
                    si = slab_n[0] % 3
                    slab_n[0] += 1
                    kk_ = 'slab%d' % si
                    P.dma('sp', kk_, wkeys, [kk_], out=slab[si].rearrange("p a b -> p (a b)"), in_=wslabs[wname][cg * nk16 + ks])
                    for s in range(2):
                        for kc in range(16):
                            MM(PS[banks[s]][:, :], lhsT_of(ks * 16 + kc, s), slab[si][:, kc, :],
                               ks == 0 and kc == 0, ks == nk16 - 1 and kc == 15,
                               [kk_] + lhs_keys, ['q%d' % banks[s]], inc=(kc == 15))
                for s in range(2):
                    evac(cg, s, PS[banks[s]], 'q%d' % banks[s])

        def evac_sq(TM, ktm):
            def ev(cg, s, ps, pk):
                V('tensor_copy', [pk], [ktm % s], out=TM[s][:, cg * 512:(cg + 1) * 512], in_=ps[:, :])
                for hf in range(2):
                    a0 = 8 + s * 16 + cg * 2 + hf
                    ACT(rtmp[0], TM[s][:, cg * 512 + hf * 256:cg * 512 + (hf + 1) * 256], AF.Square, [ktm % s, 'rt0'], ['rt0', 'ssq%d' % s],
                        accum_out=small2[:, a0:a0 + 1])
            return ev

        def post_norm_residual(TM, tmk, s):
            sq, rs = small2[:, 5:6], small2[:, 6:7]
            V('tensor_reduce', ['ssq%d' % s], ['sq5'], out=sq, in_=small2[:, 8 + s * 16:24 + s * 16], axis=AX.X, op=ALU.add)
            ACT(rs, sq, AF.Sqrt, ['sq5', 'c2'], ['rs6'], scale=1.0 / D, bias=epsc2[:, 0:1])
            V('reciprocal', ['rs6'], ['rs6'], out=rs, in_=rs)
            V('scalar_tensor_tensor', [tmk, 'rs6', 'gbc'], [tmk], out=TM, in0=TM, scalar=rs, in1=gbc, op0=ALU.mult, op1=ALU.mult)
            G('tensor_tensor', [tmk, 'xres%d' % s], ['xres%d' % s], out=xres[:, s, :], in0=xres[:, s, :], in1=TM, op=ALU.add)

        hT2_of = lambda kc, s: hT2[:, kc, s * 128:(s + 1) * 128]
        sel_d = dram("sel", [128, 4], F32)
        sel = A2.f32(4)
        P.dma('sp', 'sel', [], ['sel'], out=sel, in_=sel_d)

        def p2stop(name):
            if stop == name:
                raise _Stop()

        def phase2():
          for tt in range(NT2):
              R0 = tt * 256
              p2stop('p2a')
              P.alias(['TMc0', 'TMc1', 'lng', 'lnb', 'vln', 'cand0', 'cand1'], ['f1T'])
              P.alias(['hT2', 'tmb0', 'tmb1'], ['TMg0', 'TMg1'])
              for s in range(2):
                  P.dma('sp', 'xres%d' % s, [], ['xres%d' % s], out=xres[:, s, :], in_=xq[R0 + s * 128:R0 + (s + 1) * 128, :])
              P.dma('sp', 'lng', [], ['lng'], out=lng, in_=fp_d[:, FP_LNG:FP_LNG + 2048].partition_broadcast(128))
              P.dma('sp', 'lnb', [], ['lnb'], out=lnb, in_=fp_d[:, FP_LNB:FP_LNB + 2048].partition_broadcast(128))
              P.dma('sp', 'gbc', [], ['gbc'], out=gbc, in_=fp_d[:, FP_GMIX:FP_GMIX + 4096].partition_broadcast(128))
              for s in range(2):
                  rms_rows(xres[:, s, :], tmb[s], 'xres%d' % s, 'tmb%d' % s, small2, epsc2[:, 0:1], 'c2')
                  transposes(tmb[s], hT2, s * 128, (pp2[:, PP_GPRE:PP_GPRE + 32], 'pp2'), 'tmb%d' % s, 'hT2', identb2, 'c2', 'q%d')

              def evacC(cg, s, ps, pk):
                  ACT(TMc[s][:, cg * 512:(cg + 1) * 512], ps[:, :], AF.Gelu_apprx_tanh, [pk], ['TMc%d' % s])
              p2stop('p2b')
              tokmajor_matmul('wg', [], hT2_of, 2, ['hT2'], evacC)
              p2stop('p2c')

              for s in range(2):
                  z = TMc[s]
                  zk = 'TMc%d' % s
                  vv = z[:, 2048:4096]
                  sm, sq, rs = small2[:, 2:3], small2[:, 3:4], small2[:, 4:5]
                  V('tensor_reduce', [zk], ['sm'], out=sm, in_=vv, axis=AX.X, op=ALU.add)
                  V('tensor_scalar', ['sm'], ['sm'], out=sm, in0=sm, scalar1=1.0 / 2048, scalar2=None, op0=ALU.mult)
                  V('tensor_scalar', [zk, 'sm'], [zk], out=vv, in0=vv, scalar1=sm, scalar2=None, op0=ALU.subtract)
                  ACT(vln, vv, AF.Square, [zk], ['vln', 'sq'], accum_out=sq)
                  ACT(rs, sq, AF.Sqrt, ['sq', 'c2'], ['rsd'], scale=1.0 / 2048, bias=epsc2[:, 2:3])
                  V('reciprocal', ['rsd'], ['rsd'], out=rs, in_=rs)
                  V('scalar_tensor_tensor', [zk, 'rsd', 'lng'], [zk], out=vv, in0=vv, scalar=rs, in1=lng, op0=ALU.mult, op1=ALU.mult)
                  G('tensor_tensor', [zk, 'lnb', 'vln'], ['vln'], out=vln, in0=vv, in1=lnb, op=ALU.add)
                  p2stop('p2d1')
                  for h4 in range(4):
                      bank = h4 % 2
                      for i in range(4):
                          h = h4 * 4 + i
                          MM(PS[bank][:, i * 128:(i + 1) * 128], wsT[:, h, :], vln[:, h * 128:(h + 1) * 128], True, True,
                             ['wsT', 'vln'], ['q%d' % bank], inc=(i == 3))
                      vsl = vv[:, h4 * 512:(h4 + 1) * 512].rearrange("p (a b) -> p a b", a=4, b=128)
                      V('tensor_tensor', ['q%d' % bank, 'pp2', zk], [zk], out=vsl,
                        in0=PS[bank][:, :].rearrange("p (a b) -> p a b", a=4, b=128),
                        in1=bc3(pp2[:, PP_BS + h4 * 4:PP_BS + h4 * 4 + 4], 128), op=ALU.add)
                  p2stop('p2d2')
                  tk = 'tmb%d' % s
                  G('tensor_tensor', [zk, tk], [tk], out=tmb[s][:, 0:2048], in0=z[:, 0:2048], in1=vv, op=ALU.mult)
                  if debug:
                      V('tensor_copy', [tk, zk], [zk], out=z[:, 0:2048], in_=tmb[s][:, 0:2048])
                  yb_dst = tmb[s][:, 2048:4096].rearrange("p (g c) -> p g c", g=4)
                  if single:
                      G('memset', [tk], [tk], ap=tmb[s][:, 2048:4096], constant=0.0)
                  for jq in range(0 if single else 4):
                      cb = cand[jq % 2]
                      ck = 'cand%d' % (jq % 2)
                      tg = jq * TQ + R0 + s * 128
                      pi_, rr = tg // PR, tg % PR
                      P.dma('sp', ck, [], [ck], out=cb,
                            in_=yb_gat[pi_ * 4 * PR:(pi_ + 1) * 4 * PR, :].rearrange("(g r) c -> r g c", g=4)[rr:rr + 128, :, :])
                      if jq == 0:
                          V('tensor_scalar', [ck, 'sel', tk], [tk], out=yb_dst, in0=cb, scalar1=sel[:, 0:1], scalar2=None, op0=ALU.mult)
                      else:
                          V('scalar_tensor_tensor', [ck, 'sel', tk], [tk], out=yb_dst, in0=cb, scalar=sel[:, jq:jq + 1], in1=yb_dst,
                            op0=ALU.mult, op1=ALU.add)
                  if debug:
                      V('tensor_copy', [tk, zk], [zk], out=z[:, 2048:4096], in_=tmb[s][:, 2048:4096])
                      P.dma('sp', 'dbg3', [zk], ['dbg_ya'], out=dbg['ya'][R0 + s * 128:R0 + (s + 1) * 128, :], in_=z)
                  p2stop('p2d3')
                  transposes(tmb[s], hT2, s * 128, None, tk, 'hT2', identb2, 'c2', 'q%d')

              p2stop('p2e')
              tokmajor_matmul('wo', [], hT2_of, 2, ['hT2'], evac_sq(TMc, 'TMc%d'))
              p2stop('p2f')
              p2stop('p2f_noact')
              for s in range(2):
                  post_norm_residual(TMc[s], 'TMc%d' % s, s)
                  if debug:
                      P.dma('sp', 'dbg4', ['xres%d' % s], ['dbg_x1'], out=dbg['x1'][R0 + s * 128:R0 + (s + 1) * 128, :], in_=xres[:, s, :])

              P.dma('sp', 'gbc', ['gbc'], ['gbc'], out=gbc, in_=fp_d[:, FP_GOUT:FP_GOUT + 4096].partition_broadcast(128))
              for s in range(2):
                  rms_rows(xres[:, s, :], tmb[s], 'xres%d' % s, 'tmb%d' % s, small2, epsc2[:, 0:1], 'c2')
                  transposes(tmb[s], hT2, s * 128, (pp2[:, PP_GFFN:PP_GFFN + 32], 'pp2'), 'tmb%d' % s, 'hT2', identb2, 'c2', 'q%d')
              P.alias(['f1T'], ['TMc0', 'TMc1', 'lng', 'lnb', 'vln', 'cand0', 'cand1'])
              for fg in range(dff // 256):
                  si = slab_n[0] % 3
                  slab_n[0] += 1
                  kk_ = 'slab%d' % si
                  sl = slab[si].rearrange("p a b -> p (a b)").rearrange("p (k n) -> p k n", k=KC, n=256)
                  P.dma('sp', kk_, [], [kk_], out=slab[si].rearrange("p a b -> p (a b)"), in_=wslabs['w1'][fg])
                  bank = 2 + (fg % 4)
                  for fi in range(2):
                      for kc in range(KC):
                          MM(PS[bank][:, fi * 256:(fi + 1) * 256], sl[:, kc, fi * 128:(fi + 1) * 128], hT2[:, kc, :],
                             kc == 0, kc == KC - 1, [kk_, 'hT2'], ['q%d' % bank], inc=(kc == KC - 1))
                  for fi in range(2):
                      rt = rtmp[fi]
                      ACT(rt, PS[bank][:, fi * 256:(fi + 1) * 256], AF.Relu, ['q%d' % bank], ['rt%d' % fi])
                      G('tensor_tensor', ['rt%d' % fi], ['f1T'], out=f1T[:, fg * 2 + fi, :], in0=rt, in1=rt, op=ALU.mult)
              P.alias(['TMg0', 'TMg1'], ['hT2', 'tmb0', 'tmb1'])
              tokmajor_matmul('w2', [], lambda kc, s: f1T[:, kc, s * 128:(s + 1) * 128], dff // 2048, ['f1T'], evac_sq(TMg, 'TMg%d'))
              for s in range(2):
                  post_norm_residual(TMg[s], 'TMg%d' % s, s)
                  P.dma('sp', 'out%d' % s, ['xres%d' % s], ['out%d' % s], out=out_d[R0 + s * 128:R0 + (s + 1) * 128, :], in_=xres[:, s, :])
        try:
            phase2()
        except _Stop:
            P.barrier()
            P.replay()
            return nc
        P.wait_all('sp', ['out0', 'out1'])
        if debug:
            P.wait_all('sp', ['dbg_yb', 'dbg_ysc', 'dbg_ya', 'dbg_x1'])
        P.replay()
    return nc


def _prep_inputs(inputs, T, single=False):
    f = lambda k: np.asarray(inputs[k], np.float32)
    x = f('x')[:, :T]
    w_in = f('w_in')[0]
    mu = f('tshift_mu')[0]
    TQ = T if single else T // 4
    ws = f('gmlp_ws')[0]
    chunkcols = lambda v, n: np.ascontiguousarray(v.reshape(n, 128).T)

    def tok_units(w):
        K_ = w.shape[0]
        a = w.reshape(K_ // 2048, 16, 128, 8, 512).transpose(3, 0, 2, 1, 4)
        return np.ascontiguousarray(a).reshape(-1, 64, 8192)

    units = {}
    if f('w_out').shape[1] == D:
        units['wg'] = tok_units(w_in[:, :4096])
        units['wo'] = tok_units(f('w_out')[0])
        units['w2'] = tok_units(f('w_ff2')[0])
        units['w1'] = np.ascontiguousarray(f('w_ff1')[0].reshape(32, 128, -1, 256).transpose(2, 1, 0, 3)).reshape(-1, 64, 8192)
    maps = []
    for c in range(NCORE):
        b, g = c // 4, c % 4
        hs = slice(g * 512, (g + 1) * 512)
        base = 4096
        cols = np.concatenate([base + g * 512 + np.arange(512), base + 2048 + g * 512 + np.arange(512),
                               base + 4096 + g * 512 + np.arange(512), base + 6144 + np.arange(384)])
        w_rw = w_in[:, cols]
        wrw_h = np.ascontiguousarray(w_rw.reshape(KC, 128, 15, 128).transpose(2, 1, 0, 3)).reshape(15, 128, KC * 128)
        pp = np.zeros((128, NPP), np.float32)
        pp[:, PP_GPRE:PP_GPRE + 32] = chunkcols(f('pre_mix_g')[0], 32)
        pp[:, PP_GFFN:PP_GFFN + 32] = chunkcols(f('pre_ffn_g')[0], 32)
        pp[:, PP_MU:PP_MU + 15] = chunkcols(mu[cols - base], 15)
        pp[:, PP_W0:PP_W0 + 4] = chunkcols(f('decay_w0')[0][hs], 4)
        pp[:, PP_A0:PP_A0 + 4] = chunkcols(f('iclr_a0')[0][hs], 4)
        pp[:, PP_KK:PP_KK + 4] = chunkcols(f('k_k')[0][hs], 4)
        pp[:, PP_KA:PP_KA + 4] = chunkcols(f('k_a')[0][hs], 4)
        pp[:, PP_RK:PP_RK + 4] = chunkcols(f('r_k')[0].reshape(-1)[hs], 4)
        pp[:, PP_BS:PP_BS + 16] = f('gmlp_bs')[0].T
        fp = np.concatenate([f('lnx_g')[0][hs], f('lnx_b')[0][hs], f('gmlp_ln_g')[0], f('gmlp_ln_b')[0],
                             f('post_mix_g')[0], f('post_ffn_g')[0]])[None, :].astype(np.float32)
        lora = np.concatenate([f('decay_up')[0][:, hs], f('iclr_up')[0][:, hs]], axis=0)
        sel = np.zeros((128, 4), np.float32)
        sel[:, g] = 1.0
        m = {
            "xq": (x[b] if single else x[b, g * TQ:(g + 1) * TQ]), "wrw": wrw_h,
            "pp": pp, "fp": np.ascontiguousarray(fp), "lora_up": np.ascontiguousarray(lora),
            "gate_up": np.ascontiguousarray(f('gate_up')[0][:, hs]), "ws": ws, "sel": sel,
        }
        for nm, u in units.items():
            m[nm] = np.ascontiguousarray(u[c % 4::4]).reshape(-1, 8192)
        maps.append(m)
    return maps


_NC_CACHE = {}


def run(inputs, T, debug=False, stop=None, dff=DFF):
    key = (T, debug, stop, dff)
    if key not in _NC_CACHE:
        _NC_CACHE[key] = build_nc(T, debug, stop, dff=dff)
    nc = _NC_CACHE[key]
    maps = _prep_inputs(inputs, T)
    names = set()
    for a in nc.m.functions[0].allocations:
        if isinstance(a, mybir.MemoryLocationSet) and a.kind == "ExternalInput":
            names.add(a.memorylocations[0].name)
    maps = [{k: v for k, v in m.items() if k in names} for m in maps]
    res = run_bass_kernel_spmd(nc, maps, core_ids=list(range(NCORE)))
    TQ = T // 4
    out = np.zeros((2, T, D), np.float32)
    for c in range(NCORE):
        b, q = c // 4, c % 4
        out[b, q * TQ:(q + 1) * TQ] = res.results[c]["out"]
    return out, res


def kernel(**inputs):
    out, _ = run(inputs, 8192)
    return out
```
